# Optimizing a Trainium2 kernel written in Bass

```python
import jax, jax.numpy as jnp
from jax import lax
import numpy as np

D_MODEL = 1024
BATCH = 8
SEQ = 2048
DEPTH = 2
DEC_BATCH = 128
DEC_SEQ = 8
PAST_LEN = 16384
PAGE_SIZE = 128

N_MEM = 256
XA_HEADS = 4
XA_HEAD_DIM = D_MODEL // XA_HEADS
D_A = D_MODEL
CONV_A_WIDTH = 31
D_B = D_MODEL
LRU_BLOCKS = 8
LRU_BLOCK = D_B // LRU_BLOCKS
CONV_B_WIDTH = 4
LRU_C = 8.0
D_FF = 2816
OFF_BX = 2 * D_A
OFF_BG = OFF_BX + D_B
OFF_GA = OFF_BG + D_B
OFF_GB = OFF_GA + D_MODEL
D_IN = OFF_GB + D_MODEL
N_NORMS = 9
EPS = 1e-6

kernel_name = 'hybrid_conv_rglru_gated_decoder_step'


def rms_norm(x, g):
    xf = x.astype(jnp.float32)
    y = xf * lax.rsqrt(jnp.mean(xf * xf, axis=-1, keepdims=True) + EPS)
    return (y * g.astype(jnp.float32)).astype(x.dtype)


def layer_norm(x, g, b):
    xf = x.astype(jnp.float32)
    mu = jnp.mean(xf, axis=-1, keepdims=True)
    var = jnp.mean(jnp.square(xf - mu), axis=-1, keepdims=True)
    y = (xf - mu) * lax.rsqrt(var + EPS)
    return (y * g.astype(jnp.float32) + b.astype(jnp.float32)).astype(x.dtype)


def swiglu(x, w_in, w_out):
    gate, up = jnp.split(x @ w_in, 2, axis=-1)
    return (jax.nn.silu(gate) * up) @ w_out


def causal_dw_conv(u, buf, w, b):
    full = jnp.concatenate([buf.astype(u.dtype), u], axis=1)
    y = lax.conv_general_dilated(full, w[:, None, :].astype(u.dtype), window_strides=(1,), padding='VALID',
                                 dimension_numbers=('NWC', 'WIO', 'NWC'), feature_group_count=u.shape[-1])
    return y + b, full[:, -buf.shape[1]:]


def _lru_combine(c1, c2):
    a1, b1 = c1
    a2, b2 = c2
    return a1 * a2, a2 * b1 + b2


def rg_lru(xb, h0, w_a, b_a, w_x, b_x, lam):
    B, T, _ = xb.shape
    xf = xb.astype(jnp.float32)
    xh = xf.reshape(B, T, LRU_BLOCKS, LRU_BLOCK)
    r = jax.nn.sigmoid(jnp.einsum('btni,nij->btnj', xh, w_a.astype(jnp.float32)).reshape(B, T, D_B) + b_a)
    i = jax.nn.sigmoid(jnp.einsum('btni,nij->btnj', xh, w_x.astype(jnp.float32)).reshape(B, T, D_B) + b_x)
    log_a = -LRU_C * r * jax.nn.softplus(-lam.astype(jnp.float32))
    a = jnp.exp(log_a)
    u = jnp.sqrt(-jnp.expm1(2.0 * log_a)) * (i * xf)
    a_cum, b_cum = lax.associative_scan(_lru_combine, (a, u), axis=1)
    h = a_cum * h0.astype(jnp.float32)[:, None, :] + b_cum
    return h.astype(xb.dtype), h[:, -1].astype(h0.dtype)


def mixer(u, buf_a, buf_b, h0, p, l):
    proj = u @ p['w_in'][l] + p['b_in'][l]
    a_in, b_x, b_g, g_a, g_b = jnp.split(proj, [OFF_BX, OFF_BG, OFF_GA, OFF_GB], axis=-1)
    a_glu = a_in[..., :D_A] * jax.nn.sigmoid(a_in[..., D_A:])
    a_conv, new_buf_a = causal_dw_conv(a_glu, buf_a, p['conv_a_w'][l], p['conv_a_b'][l])
    y_a = jax.nn.silu(layer_norm(a_conv, p['conv_ln_g'][l], p['conv_ln_b'][l])) @ p['w_a_out'][l]
    b_conv, new_buf_b = causal_dw_conv(b_x, buf_b, p['conv_b_w'][l], p['conv_b_b'][l])
    h, h_last = rg_lru(b_conv, h0, p['lru_w_a'][l], p['lru_b_a'][l], p['lru_w_x'][l], p['lru_b_x'][l], p['lru_lambda'][l])
    y_b = (h * jax.nn.gelu(b_g)) @ p['w_b_out'][l]
    merged = jax.nn.sigmoid(g_a) * y_a + jax.nn.sigmoid(g_b) * y_b
    return merged @ p['w_out'][l], new_buf_a, new_buf_b, h_last


def mem_kv(mem, g, w_k, w_v):
    m = rms_norm(mem, g)
    return jnp.einsum('bmd,dhk->bmhk', m, w_k), jnp.einsum('bmd,dhk->bmhk', m, w_v)


def cross_attn(u, k, v, w_q, w_o):
    q = jnp.einsum('btd,dhk->bthk', u, w_q)
    s = jnp.einsum('bthk,bmhk->bhtm', q.astype(jnp.float32), k.astype(jnp.float32)) * (XA_HEAD_DIM ** -0.5)
    pr = jax.nn.softmax(s, axis=-1).astype(v.dtype)
    o = jnp.einsum('bhtm,bmhk->bthk', pr, v)
    return jnp.einsum('bthk,hkd->btd', o, w_o)


def trunk(x, mem_k, mem_v, buf_a, buf_b, h0, p):
    new_a, new_b, new_h = [], [], []
    for l in range(DEPTH):
        g = p['norm_g'][l]
        x = x + 0.5 * rms_norm(swiglu(rms_norm(x, g[0]), p['ffn1_w_in'][l], p['ffn1_w_out'][l]), g[1])
        m, nba, nbb, hl = mixer(rms_norm(x, g[2]), buf_a[l], buf_b[l], h0[l], p, l)
        x = x + rms_norm(m, g[3])
        x = x + rms_norm(cross_attn(rms_norm(x, g[4]), mem_k[l], mem_v[l], p['xa_w_q'][l], p['xa_w_o'][l]), g[5])
        x = x + 0.5 * rms_norm(swiglu(rms_norm(x, g[7]), p['ffn2_w_in'][l], p['ffn2_w_out'][l]), g[8])
        new_a.append(nba)
        new_b.append(nbb)
        new_h.append(hl)
    return x, jnp.stack(new_a), jnp.stack(new_b), jnp.stack(new_h)


def setup_inputs(seed: int = 0) -> dict:
    key = jax.random.key(seed)
    ks = jax.random.split(key, 34)
    f32 = jnp.float32
    D = D_MODEL

    def nrm(k, shape, scale):
        return scale * jax.random.normal(k, shape, f32)

    a0 = jax.random.uniform(ks[21], (DEPTH, D_B), f32, 0.9, 0.999)
    s0 = a0 ** (1.0 / LRU_C)
    return {
        'x_prompt': nrm(ks[0], (BATCH, SEQ, D), 1.0),
        'x_sample': nrm(ks[1], (DEC_BATCH, DEC_SEQ, D), 1.0),
        'mem_prompt': nrm(ks[2], (BATCH, N_MEM, D), 1.0),
        'cache_mem_k': nrm(ks[3], (DEPTH, DEC_BATCH, N_MEM, XA_HEADS, XA_HEAD_DIM), 1.0),
        'cache_mem_v': nrm(ks[4], (DEPTH, DEC_BATCH, N_MEM, XA_HEADS, XA_HEAD_DIM), 1.0),
        'state_conv_a': nrm(ks[5], (DEPTH, DEC_BATCH, CONV_A_WIDTH - 1, D_A), 0.5),
        'state_conv_b': nrm(ks[6], (DEPTH, DEC_BATCH, CONV_B_WIDTH - 1, D_B), 1.0),
        'state_lru_h': nrm(ks[7], (DEPTH, DEC_BATCH, D_B), 0.5),
        'w_in': nrm(ks[8], (DEPTH, D, D_IN), D ** -0.5),
        'b_in': nrm(ks[9], (DEPTH, D_IN), 0.01),
        'conv_a_w': nrm(ks[10], (DEPTH, CONV_A_WIDTH, D_A), CONV_A_WIDTH ** -0.5),
        'conv_a_b': nrm(ks[11], (DEPTH, D_A), 0.01),
        'conv_ln_g': 1.0 + nrm(ks[12], (DEPTH, D_A), 0.02),
        'conv_ln_b': nrm(ks[13], (DEPTH, D_A), 0.01),
        'w_a_out': nrm(ks[14], (DEPTH, D_A, D), D_A ** -0.5),
        'conv_b_w': nrm(ks[15], (DEPTH, CONV_B_WIDTH, D_B), CONV_B_WIDTH ** -0.5),
        'conv_b_b': nrm(ks[16], (DEPTH, D_B), 0.01),
        'lru_w_a': nrm(ks[17], (DEPTH, LRU_BLOCKS, LRU_BLOCK, LRU_BLOCK), LRU_BLOCK ** -0.5),
        'lru_b_a': nrm(ks[18], (DEPTH, D_B), 0.01),
        'lru_w_x': nrm(ks[19], (DEPTH, LRU_BLOCKS, LRU_BLOCK, LRU_BLOCK), LRU_BLOCK ** -0.5),
        'lru_b_x': nrm(ks[20], (DEPTH, D_B), 0.01),
        'lru_lambda': jnp.log(s0) - jnp.log1p(-s0),
        'w_b_out': nrm(ks[22], (DEPTH, D_B, D), D_B ** -0.5),
        'w_out': nrm(ks[23], (DEPTH, D, D), D ** -0.5),
        'xa_w_q': nrm(ks[24], (DEPTH, D, XA_HEADS, XA_HEAD_DIM), D ** -0.5),
        'xa_w_k': nrm(ks[25], (DEPTH, D, XA_HEADS, XA_HEAD_DIM), D ** -0.5),
        'xa_w_v': nrm(ks[26], (DEPTH, D, XA_HEADS, XA_HEAD_DIM), D ** -0.5),
        'xa_w_o': nrm(ks[27], (DEPTH, XA_HEADS, XA_HEAD_DIM, D), D ** -0.5),
        'ffn1_w_in': nrm(ks[28], (DEPTH, D, 2 * D_FF), D ** -0.5),
        'ffn1_w_out': nrm(ks[29], (DEPTH, D_FF, D), D_FF ** -0.5),
        'ffn2_w_in': nrm(ks[30], (DEPTH, D, 2 * D_FF), D ** -0.5),
        'ffn2_w_out': nrm(ks[31], (DEPTH, D_FF, D), D_FF ** -0.5),
        'norm_g': 1.0 + nrm(ks[32], (DEPTH, N_NORMS, D), 0.02),
    }


def reference(x_prompt, x_sample, mem_prompt, cache_mem_k, cache_mem_v, state_conv_a, state_conv_b, state_lru_h,
              w_in, b_in, conv_a_w, conv_a_b, conv_ln_g, conv_ln_b, w_a_out, conv_b_w, conv_b_b,
              lru_w_a, lru_b_a, lru_w_x, lru_b_x, lru_lambda, w_b_out, w_out,
              xa_w_q, xa_w_k, xa_w_v, xa_w_o, ffn1_w_in, ffn1_w_out, ffn2_w_in, ffn2_w_out, norm_g):
    p = dict(w_in=w_in, b_in=b_in, conv_a_w=conv_a_w, conv_a_b=conv_a_b, conv_ln_g=conv_ln_g, conv_ln_b=conv_ln_b,
             w_a_out=w_a_out, conv_b_w=conv_b_w, conv_b_b=conv_b_b, lru_w_a=lru_w_a, lru_b_a=lru_b_a,
             lru_w_x=lru_w_x, lru_b_x=lru_b_x, lru_lambda=lru_lambda, w_b_out=w_b_out, w_out=w_out,
             xa_w_q=xa_w_q, xa_w_o=xa_w_o, ffn1_w_in=ffn1_w_in, ffn1_w_out=ffn1_w_out,
             ffn2_w_in=ffn2_w_in, ffn2_w_out=ffn2_w_out, norm_g=norm_g)
    mk, mv = [], []
    for l in range(DEPTH):
        k, v = mem_kv(mem_prompt, norm_g[l, 6], xa_w_k[l], xa_w_v[l])
        mk.append(k)
        mv.append(v)
    mem_k_prompt = jnp.stack(mk)
    mem_v_prompt = jnp.stack(mv)
    bp = x_prompt.shape[0]
    zero_a = jnp.zeros((DEPTH, bp, CONV_A_WIDTH - 1, D_A), x_prompt.dtype)
    zero_b = jnp.zeros((DEPTH, bp, CONV_B_WIDTH - 1, D_B), x_prompt.dtype)
    zero_h = jnp.zeros((DEPTH, bp, D_B), state_lru_h.dtype)
    y_prompt, conv_a_prompt, conv_b_prompt, lru_h_prompt = trunk(
        x_prompt, mem_k_prompt, mem_v_prompt, zero_a, zero_b, zero_h, p)
    y_sample, conv_a_sample, conv_b_sample, lru_h_sample = trunk(
        x_sample, cache_mem_k, cache_mem_v, state_conv_a, state_conv_b, state_lru_h, p)
    return (y_prompt, y_sample, mem_k_prompt, mem_v_prompt, conv_a_prompt, conv_b_prompt, lru_h_prompt,
            conv_a_sample, conv_b_sample, lru_h_sample)
```

```python
import numpy as np
import concourse.bass as bass
import concourse.mybir as mybir
from concourse.bass_utils import run_bass_kernel_spmd

F32 = mybir.dt.float32
BF16 = mybir.dt.bfloat16
AF = mybir.ActivationFunctionType
ALU = mybir.AluOpType
AX = mybir.AxisListType

D = 1024
NCH = 8
DFF = 2816
NFF = 22
DIN = 6144
SEQ = 2048
NSEQ_S = 16
DS = 8
NGRP = 2
PG = 1024
SQG = 8
SG = SQG * DS
TOK = PG + SG
BPAD = 30
BCOLS = TOK + BPAD
NMEM = 256
NCORES = 8
EPS = 1e-6
DEPTH = 2
TOKC = NGRP * TOK

def R_NG(l, i): return l * 9 + i
def R_BIN(l, j): return 18 + l * 6 + j
def R_CAW(l, w): return 30 + l * 31 + w
def R_CAB(l): return 92 + l
def R_LNG(l): return 94 + l
def R_LNB(l): return 96 + l
def R_CBW(l, w): return 98 + l * 4 + w
def R_CBB(l): return 106 + l
def R_LBA(l): return 108 + l
def R_LBX(l): return 110 + l
def R_LAM(l): return 112 + l
NROW = 114

import os
XA_SKIP = set(os.environ.get('XA_SKIP', '').split(','))
STAGES = ('ffn1', 'mix', 'xa', 'ffn2')
N_LAYERS = DEPTH
NSLOT = 4
SLOT_ELEMS = NFF * 128


def _keys(regs):
    out = []
    for r in regs:
        if isinstance(r, tuple) and len(r) == 2 and isinstance(r[1], list):
            out.extend(r[1])
        elif isinstance(r, list):
            out.extend(r)
        else:
            out.append(r)
    return out


class Slot:
    def __init__(self, sem):
        self.sem = sem
        self.count = 0


class Prog:
    def __init__(self, nc):
        self.nc = nc
        self.planning = False
        self.csem = {e: nc.alloc_semaphore('c_' + e) for e in ('pe', 'act', 'dve', 'pool')}
        self.cnt = {e: 0 for e in self.csem}
        self.ops = {e: [] for e in ('pe', 'act', 'dve', 'pool', 'sp')}
        self.seen = {e: {} for e in self.ops}
        self.lw = {}
        self.rd = {}
        self.nsem = 0

    def slot(self, name):
        return Slot(self.nc.alloc_semaphore(name))

    def _waits(self, eng, rk, wk):
        deps = {}

        def add(t):
            if t is None:
                return
            n = t[0].name
            if n not in deps or deps[n][1] < t[1]:
                deps[n] = t
        for k in rk:
            add(self.lw.get(k))
        for k in wk:
            add(self.lw.get(k))
            d = self.rd.get(k)
            if d:
                for t in d.values():
                    add(t)
        out = []
        own = self.csem.get(eng)
        for n, (sem, v) in deps.items():
            if eng == 'pe' and sem is own:
                continue
            if self.seen[eng].get(n, 0) >= v:
                continue
            self.seen[eng][n] = v
            out.append((sem, v))
        return out

    def _reg(self, tok, rk, wk):
        for k in wk:
            self.lw[k] = tok
            self.rd[k] = {}
        n = tok[0].name
        for k in rk:
            d = self.rd.setdefault(k, {})
            if n not in d or d[n][1] < tok[1]:
                d[n] = tok

    def op(self, eng, fn, reads=(), writes=(), inc=True):
        if self.planning:
            return
        rk = _keys(reads)
        wk = _keys(writes)
        psr = [k for k in rk if k[0] == 'ps']
        if psr:
            wk = wk + psr
            rk = [k for k in rk if k[0] != 'ps']
        waits = self._waits(eng, rk, wk)
        sem = self.csem[eng]
        if inc:
            self.cnt[eng] += 1
            tok = (sem, self.cnt[eng])
        else:
            tok = (sem, self.cnt[eng] + 1)
        self.ops[eng].append((waits, fn, (sem, 1) if inc else None))
        self._reg(tok, rk, wk)

    def dma(self, q, slot, out_ap, in_ap, reads=(), writes=()):
        if self.planning:
            return
        rk = _keys(reads)
        wk = _keys(writes)
        waits = self._waits(q, rk, wk)
        slot.count += 16
        tok = (slot.sem, slot.count)
        self.ops[q].append((waits, (lambda e, o=out_ap, i=in_ap: e.dma_start(out=o, in_=i)), (slot.sem, 16)))
        self._reg(tok, rk, wk)

    def dma_group(self, q, slot, pairs, reads=(), writes=()):
        if self.planning:
            return
        rk = _keys(reads)
        wk = _keys(writes)
        waits = self._waits(q, rk, wk)
        for j, (o_, i_) in enumerate(pairs):
            slot.count += 16
            self.ops[q].append((waits if j == 0 else [], (lambda e, o=o_, i=i_: e.dma_start(out=o, in_=i)), (slot.sem, 16)))
        tok = (slot.sem, slot.count)
        self._reg(tok, rk, wk)

    def final_wait(self, q, slots):
        waits = [(s.sem, s.count) for s in slots if s.count > 0]
        self.ops[q].append((waits, None, None))

    def emit(self):
        nc = self.nc
        ops = self.ops

        def mk(name):
            def body(e):
                for waits, fn, inc in ops[name]:
                    for sem, v in waits:
                        e.wait_ge(sem, v)
                    if fn is None:
                        continue
                    ins = fn(e)
                    if inc is not None:
                        ins.then_inc(inc[0], inc[1])
            return body
        with nc.Block() as block:
            block.tensor(mk('pe'))
            block.scalar(mk('act'))
            block.vector(mk('dve'))
            block.gpsimd(mk('pool'))
            block.sync(mk('sp'))


class Buf:
    def __init__(self, nc, name, rows, cols, dt, page=64):
        self.name = name
        self.rows = rows
        self.cols = cols
        self.page = page
        self.dt = dt
        self.t = nc.alloc_sbuf_tensor(name, [128, rows * cols], dt)

    def keys(self, a, n):
        return [(self.name, p) for p in range(a // self.page, (a + n - 1) // self.page + 1)]

    def f(self, a, n, parts=128):
        return (self.t[0:parts, a:a + n], self.keys(a, n))

    def r(self, row, c0, n, parts=128):
        return self.f(row * self.cols + c0, n, parts)

    def g(self, a, n, pattern, parts=128, **kw):
        return (self.t[0:parts, a:a + n].rearrange(pattern, **kw), self.keys(a, n))


class Tile:
    def __init__(self, idx, c0, n, kind):
        self.idx = idx
        self.c0 = c0
        self.n = n
        self.kind = kind


TILES = [Tile(0, 0, 512, 'P'), Tile(1, 512, 512, 'P'), Tile(2, 1024, SG, 'S')]


def build_program(stages=STAGES, n_layers=N_LAYERS, n_groups=NGRP):
    nc = bass.Bass("TRN2", target_bir_lowering=False)
    P = Prog(nc)

    def dram_in(name, shape):
        return nc.dram_tensor(name, list(shape), F32, kind="ExternalInput").ap()

    def dram_out(name, shape):
        return nc.dram_tensor(name, list(shape), F32, kind="ExternalOutput").ap()

    xin = dram_in("xin", [128, NCH, TOKC])
    memT = dram_in("memT", [128, NCH, NMEM])
    kcT = dram_in("kcT", [DEPTH, NSEQ_S, 128, 8, NMEM])
    vc = dram_in("vc", [DEPTH, NSEQ_S, 128, 2, D])
    sca = dram_in("sca", [DEPTH, 128, NCH, NSEQ_S, 30])
    scb = dram_in("scb", [DEPTH, 128, NCH, NSEQ_S, 3])
    slh = dram_in("slh", [DEPTH, 128, NCH, NSEQ_S])
    consts_d = dram_in("consts", [128, NCH, NROW])
    w_in = dram_in("w_in", [DEPTH, D, DIN])
    w_a_out = dram_in("w_a_out", [DEPTH, D, D])
    w_b_out = dram_in("w_b_out", [DEPTH, D, D])
    w_out = dram_in("w_out", [DEPTH, D, D])
    lru_w_a = dram_in("lru_w_a", [DEPTH, 8, 128, 128])
    lru_w_x = dram_in("lru_w_x", [DEPTH, 8, 128, 128])
    xa_w_q = dram_in("xa_w_q", [DEPTH, D, D])
    xa_w_k = dram_in("xa_w_k", [DEPTH, D, D])
    xa_w_v = dram_in("xa_w_v", [DEPTH, D, D])
    xa_w_o = dram_in("xa_w_o", [DEPTH, D, D])
    ffn1_w_in = dram_in("ffn1_w_in", [DEPTH, D, 2 * DFF])
    ffn1_w_out = dram_in("ffn1_w_out", [DEPTH, DFF, D])
    ffn2_w_in = dram_in("ffn2_w_in", [DEPTH, D, 2 * DFF])
    ffn2_w_out = dram_in("ffn2_w_out", [DEPTH, DFF, D])

    yout = dram_out("yout", [128, NCH, TOKC])
    mk_o = dram_out("mk", [DEPTH, NMEM, D])
    mv_o = dram_out("mv", [DEPTH, NMEM, D])
    ca_p = dram_out("ca_p", [DEPTH, 128, NCH, 30])
    cb_p = dram_out("cb_p", [DEPTH, 128, NCH, 3])
    lh_p = dram_out("lh_p", [DEPTH, 128, NCH])
    ca_s = dram_out("ca_s", [DEPTH, 128, NCH, NSEQ_S, 30])
    cb_s = dram_out("cb_s", [DEPTH, 128, NCH, NSEQ_S, 3])
    lh_s = dram_out("lh_s", [DEPTH, 128, NCH, NSEQ_S])

    X = Buf(nc, "X", NCH, TOK, F32)
    U = Buf(nc, "U", NCH, TOK, BF16)
    B = Buf(nc, "B", 24, BCOLS, BF16)
    S = Buf(nc, "S", 1, NCH * TOK, F32, page=64)
    RING = Buf(nc, "RING", NSLOT, SLOT_ELEMS, BF16, page=SLOT_ELEMS)
    DIAGA = Buf(nc, "DIAGA", 1, 31 * 128, BF16, page=128)
    DIAGB = Buf(nc, "DIAGB", 1, 4 * 128, BF16, page=4 * 128)
    AGS = Buf(nc, "AGS", NCH, SQG * 38, BF16, page=SQG * 38)
    BXS = Buf(nc, "BXS", NCH, SQG * 11, BF16, page=SQG * 11)
    LRW = Buf(nc, "LRW", 2, NCH * 128, BF16, page=128)
    SQ = Buf(nc, "SQ", 4, 512, BF16, page=512)
    RSTD = Buf(nc, "RSTD", 2, 512, F32, page=512)
    TMP = Buf(nc, "TMP", 4, 512, F32, page=512)
    CONST = Buf(nc, "CONST", NCH, NROW, F32, page=NCH * NROW)
    DER = Buf(nc, "DER", 1, 96, F32, page=96)
    IDB = Buf(nc, "IDB", 1, 128, BF16, page=128)
    IDF = Buf(nc, "IDF", 1, 128, F32, page=128)
    ONES = Buf(nc, "ONES", 1, 128, BF16, page=128)
    AGF = Buf(nc, "AGF", NCH, SG + 30, F32, page=SG + 30)
    BXF = Buf(nc, "BXF", NCH, SG + 3, F32, page=SG + 3)
    HLB = Buf(nc, "HLB", NCH, 16, F32, page=16)
    H0B = Buf(nc, "H0B", NCH, 8, F32, page=8)
    HLP = Buf(nc, "HLP", 1, NCH, F32, page=NCH)
    CARA = Buf(nc, "CARA", DEPTH * NCH, 30, BF16, page=30)
    CARB = Buf(nc, "CARB", DEPTH * NCH, 4, BF16, page=4)
    CARH = Buf(nc, "CARH", DEPTH, NCH, F32, page=NCH)
    SMALL = Buf(nc, "SMALL", 1, 64, F32, page=8)

    psum = [nc.alloc_psum_tensor(f"ps{i}", [128, 512], F32) for i in range(8)]
    pstate = {'i': 0, 'res': set()}

    def bank():
        while True:
            i = pstate['i']
            pstate['i'] = (i + 1) % 8
            if i not in pstate['res']:
                return (psum[i], [('ps', i)])

    def reserve(n):
        bs = [bank() for _ in range(n)]
        for b in bs:
            pstate['res'].add(b[1][0][1])
        return bs

    def release(bs):
        for b in bs:
            pstate['res'].discard(b[1][0][1])

    def act(out, in_, func, bias=None, scale=None, accum=None, extra_r=()):
        kw = {}
        if bias is not None:
            kw['bias'] = bias[0] if isinstance(bias, tuple) else bias
        if scale is not None:
            kw['scale'] = scale[0] if isinstance(scale, tuple) else scale
        if accum is not None:
            kw['accum_out'] = accum[0]
        rds = [in_] + [x for x in (bias, scale) if isinstance(x, tuple)] + list(extra_r)
        wrs = [out] + ([accum] if accum is not None else [])
        P.op('act', (lambda e, o=out[0], i=in_[0], f=func, k=kw: e.activation(out=o, in_=i, func=f, **k)),
             reads=rds, writes=wrs)

    def tt(eng, out, in0, in1, op):
        P.op(eng, (lambda e, o=out[0], a=in0[0], b=in1[0], p=op: e.tensor_tensor(out=o, in0=a, in1=b, op=p)),
             reads=[in0, in1], writes=[out])

    def stt(eng, out, in0, scalar, in1, op0, op1):
        sc = scalar[0] if isinstance(scalar, tuple) else scalar
        rds = [in0, in1] + ([scalar] if isinstance(scalar, tuple) else [])
        P.op(eng, (lambda e, o=out[0], a=in0[0], s=sc, b=in1[0], p0=op0, p1=op1:
                   e.scalar_tensor_tensor(out=o, in0=a, scalar=s, in1=b, op0=p0, op1=p1)),
             reads=rds, writes=[out])

    def ts(eng, out, in0, s1, op0, s2=None, op1=None):
        a1 = s1[0] if isinstance(s1, tuple) else s1
        a2 = s2[0] if isinstance(s2, tuple) else s2
        rds = [in0] + [x for x in (s1, s2) if isinstance(x, tuple)]
        if op1 is None:
            fn = (lambda e, o=out[0], a=in0[0]: e.tensor_scalar(out=o, in0=a, scalar1=a1, scalar2=None, op0=op0))
        else:
            fn = (lambda e, o=out[0], a=in0[0]: e.tensor_scalar(out=o, in0=a, scalar1=a1, scalar2=a2, op0=op0, op1=op1))
        P.op(eng, fn, reads=rds, writes=[out])

    def cp(eng, out, in_):
        if eng == 'act':
            act(out, in_, AF.Copy)
        else:
            P.op(eng, (lambda e, o=out[0], i=in_[0]: e.tensor_copy(out=o, in_=i)), reads=[in_], writes=[out])

    def recip(out, in_, fast=False):
        if fast:
            P.op('dve', (lambda e, o=out[0], i=in_[0]: e.reciprocal_approx_fast(out=o, in_=i)), reads=[in_], writes=[out])
        else:
            P.op('dve', (lambda e, o=out[0], i=in_[0]: e.reciprocal(out=o, in_=i)), reads=[in_], writes=[out])

    def memset(eng, out, val):
        P.op(eng, (lambda e, o=out[0], v=val: e.memset(o, v)), writes=[out])

    def mm(out, lhsT, rhs, start, stop, inc=None):
        if inc is None:
            inc = stop
        P.op('pe', (lambda e, o=out[0], l=lhsT[0], r=rhs[0], s0=start, s1=stop:
                    e.matmul(o, lhsT=l, rhs=r, start=s0, stop=s1)),
             reads=[lhsT, rhs], writes=[out], inc=inc)

    def transpose(out, in_, ident):
        P.op('pe', (lambda e, o=out[0], i=in_[0], d=ident[0]: e.transpose(out=o, in_=i, identity=d)),
             reads=[in_, ident], writes=[out], inc=True)

    def sl(reg, *idx):
        return (reg[0][idx], reg[1])

    def cst(c, row):
        ap, k = CONST.f(c * NROW + row, 1)
        return (ap, k)

    def der(i):
        return DER.f(i, 1)

    D_EPS, D_ONE = 0, 1
    def D_GH(l, i, c): return 2 + (l * 2 + i) * 8 + c
    def D_CL(l, c): return 40 + l * 8 + c
    def D_CL2(l, c): return 60 + l * 8 + c
    D_TMP = 80

    tmp_state = {'i': 0, 'sq': 0, 'rs': 0}

    def tmp(n):
        i = tmp_state['i']
        tmp_state['i'] = (i + 1) % 4
        return TMP.r(i, 0, n)

    def sqt(n):
        i = tmp_state['sq']
        tmp_state['sq'] = (i + 1) % 4
        return SQ.r(i, 0, n)

    def rst(n):
        i = tmp_state['rs']
        tmp_state['rs'] = (i + 1) % 2
        return RSTD.r(i, 0, n)

    class WS:
        def __init__(self):
            self.plan = []
            self.slots = [P.slot(f"w{i}") for i in range(NSLOT)]
            self.issued = 0
            self.consumed = 0

        def get(self, wap, l, c0, cc, kk):
            desc = (wap, l, c0, cc, kk)
            if P.planning:
                self.plan.append(desc)
                return None
            i = self.consumed
            self.consumed += 1
            pd = self.plan[i]
            assert pd[1:] == desc[1:] and pd[0] is wap, (i, pd[1:], desc[1:])
            lim = min(len(self.plan), i + NSLOT - 1)
            while self.issued < lim:
                self._issue(self.issued)
                self.issued += 1
            s = i % NSLOT
            reg = RING.g(s * SLOT_ELEMS, kk * cc, "p (k m) -> p k m", m=cc)
            return reg

        def _issue(self, j):
            wap, l, c0, cc, kk = self.plan[j]
            s = j % NSLOT
            reg = RING.g(s * SLOT_ELEMS, kk * cc, "p (k m) -> p k m", m=cc)
            src = wap[l][:, c0:c0 + cc].rearrange("(k p) m -> p k m", p=128)
            P.dma('pool', self.slots[s], reg[0], src, writes=[reg])

    ws = WS()

    def wv(W, k, m0, mc):
        return (W[0][:, k, m0:m0 + mc], W[1])

    def Yr(c, c0, n):
        return S.f(c * TOK + c0, n)

    def init_consts():
        s_c = P.slot("s_const")
        P.dma('sp', s_c, CONST.t[:, :].rearrange("p (c r) -> p c r", r=NROW), consts_d, writes=[CONST.f(0, NCH * NROW)])
        idf = IDF.f(0, 128)
        memset('pool', idf, 0.0)
        P.op('pool', (lambda e, o=idf[0]: e.affine_select(out=o, in_=o, pattern=[[-1, 128]], compare_op=ALU.not_equal,
                                                            fill=1.0, base=0, channel_multiplier=1)),
             reads=[idf], writes=[idf])
        cp('dve', IDB.f(0, 128), idf)
        memset('dve', ONES.f(0, 128), 1.0 / 1024.0)
        memset('dve', der(D_EPS), EPS)
        memset('dve', der(D_ONE), 1.0)
        for l in range(DEPTH):
            for i, gi in enumerate((1, 8)):
                for c in range(NCH):
                    ts('dve', der(D_GH(l, i, c)), cst(c, R_NG(l, gi)), 0.5, ALU.mult)
            for c in range(NCH):
                t1 = der(D_TMP)
                act(t1, cst(c, R_LAM(l)), AF.Exp, scale=-1.0)
                act(t1, t1, AF.Ln, bias=der(D_ONE))
                ts('dve', der(D_CL(l, c)), t1, -8.0, ALU.mult)
                ts('dve', der(D_CL2(l, c)), t1, -16.0, ALU.mult)
        memset('dve', CARA.f(0, DEPTH * NCH * 30), 0.0)
        memset('dve', CARB.f(0, DEPTH * NCH * 4), 0.0)
        memset('dve', CARH.f(0, DEPTH * NCH), 0.0)

    def rmsnorm_U(l, gi, tiles):
        for T in tiles:
            c0, n = T.c0, T.n
            ms = bank()
            for c in range(NCH):
                sq = sqt(n)
                act(sq, X.r(c, c0, n), AF.Square)
                mm(sl(ms, slice(None), slice(0, n)), ONES.f(0, 128), sq, c == 0, c == NCH - 1, inc=True)
            rs = rst(n)
            act(rs, sl(ms, slice(None), slice(0, n)), AF.Ln, bias=der(D_EPS))
            act(rs, rs, AF.Exp, scale=-0.5)
            for c in range(NCH):
                stt('dve', U.r(c, c0, n), X.r(c, c0, n), cst(c, R_NG(l, gi)), rs, ALU.mult, ALU.mult)

    pend_stats = []

    def flush_stats(keep=0):
        while len(pend_stats) > keep:
            st, n, sq, m = pend_stats.pop(0)
            mm(sl(st, slice(None), slice(0, n)), ONES.f(0, 128), sq, m == 0, m == NCH - 1, inc=True)

    def evac_y(m, T, pyn, stats):
        flush_stats(keep=2)
        act(Yr(m, T.c0, T.n), pyn, AF.Copy)
        sq = sqt(T.n)
        act(sq, pyn, AF.Square)
        pend_stats.append((stats[T.idx], T.n, sq, m))

    def post_residual(l, gscale_fn, tiles, stats):
        flush_stats()
        for T in tiles:
            c0, n = T.c0, T.n
            ms = stats[T.idx]
            rs = rst(n)
            act(rs, sl(ms, slice(None), slice(0, n)), AF.Ln, bias=der(D_EPS))
            act(rs, rs, AF.Exp, scale=-0.5)
            for c in range(NCH):
                t = tmp(n)
                tt('dve', t, Yr(c, c0, n), rs, ALU.mult)
                stt('dve', X.r(c, c0, n), t, gscale_fn(c), X.r(c, c0, n), ALU.mult, ALU.add)

    def proj_out(wap, l, in_row_fn, tiles, kk=NCH, cc=256):
        nsl = D // cc
        per = cc // 128
        ntail = 2 if cc == 128 else 1

        def group(W, mi, m, T):
            py = bank()
            pyn = sl(py, slice(None), slice(0, T.n))
            for k in range(kk):
                mm(pyn, wv(W, k, mi * 128, 128), in_row_fn(k, T), k == 0, k == kk - 1)
            return pyn
        for s_ in range(nsl - ntail):
            W = ws.get(wap, l, s_ * cc, cc, kk)
            if P.planning:
                continue
            for mi in range(per):
                m = s_ * per + mi
                for T in tiles:
                    yield m, T, group(W, mi, m, T)
        Wt = [(s_, ws.get(wap, l, s_ * cc, cc, kk)) for s_ in range(nsl - ntail, nsl)]
        if P.planning:
            return
        for T in tiles:
            for s_, W in Wt:
                for mi in range(per):
                    m = s_ * per + mi
                    yield m, T, group(W, mi, m, T)

    def ffn(l, wi, wo, gpre, gpost_i, tiles):
        rmsnorm_U(l, gpre, tiles)
        for j2 in range(NFF // 2):
            Wg = ws.get(wi, l, 256 * j2, 256, NCH)
            Wu = ws.get(wi, l, DFF + 256 * j2, 256, NCH)
            if P.planning:
                continue
            for jj in range(2):
                j = 2 * j2 + jj
                for T in tiles:
                    c0, n = T.c0, T.n
                    pg = bank()
                    pu = bank()
                    pgn = sl(pg, slice(None), slice(0, n))
                    pun = sl(pu, slice(None), slice(0, n))
                    for k in range(NCH):
                        mm(pgn, wv(Wg, k, jj * 128, 128), U.r(k, c0, n), k == 0, k == NCH - 1)
                    for k in range(NCH):
                        mm(pun, wv(Wu, k, jj * 128, 128), U.r(k, c0, n), k == 0, k == NCH - 1)
                    t = tmp(n)
                    act(t, pgn, AF.Silu)
                    tt('dve', B.r(j, BPAD + c0, n), t, pun, ALU.mult)
        stats = reserve(3)
        for m, T, pyn in proj_out(wo, l, lambda k, T: B.r(k, BPAD + T.c0, T.n), tiles, kk=NFF, cc=128):
            if P.planning:
                continue
            evac_y(m, T, pyn, stats)
        if not P.planning:
            post_residual(l, lambda c: der(D_GH(l, gpost_i, c)), tiles, stats)
        release(stats)


    s_x = P.slot("s_x")
    s_y = P.slot("s_y")
    out_slots = [s_y]

    def scan(out, a, u, init):
        ini = init[0] if isinstance(init, tuple) else init
        rds = [a, u] + ([init] if isinstance(init, tuple) else [])
        P.op('dve', (lambda e, o=out[0], d0=a[0], d1=u[0], i=ini: e.tensor_tensor_scan(
            out=o, data0=d0, data1=d1, initial=i, op0=ALU.mult, op1=ALU.add)), reads=rds, writes=[out])

    def ags(c):
        return AGS.g(c * SQG * 38, SQG * 38, "p (s t) -> p s t", t=38)

    def bxs(c):
        return BXS.g(c * SQG * 11, SQG * 11, "p (s t) -> p s t", t=11)

    def v3(reg, t=DS):
        return (reg[0].rearrange("p (s t) -> p s t", t=t), reg[1])

    def proj_w_in(l, blk, tiles):
        for s_ in range(4):
            W = ws.get(w_in, l, blk * D + s_ * 256, 256, NCH)
            if P.planning:
                continue
            for mi in range(2):
                m = s_ * 2 + mi
                for T in tiles:
                    ps = bank()
                    psn = sl(ps, slice(None), slice(0, T.n))
                    for k in range(NCH):
                        mm(psn, wv(W, k, mi * 128, 128), U.r(k, T.c0, T.n), k == 0, k == NCH - 1)
                    yield m, T, psn

    s_sa = P.slot("s_sa"); s_sb = P.slot("s_sb"); s_sh = P.slot("s_sh"); s_lrw = P.slot("s_lrw")
    s_o1 = P.slot("s_o1"); s_o2 = P.slot("s_o2"); s_o3 = P.slot("s_o3"); s_cp = P.slot("s_cp")
    out_slots.extend([s_o1, s_o2, s_o3, s_cp])

    SL_XF, SL_A2M = 0, 1

    def Sf(slot, T):
        return S.f(slot * TOK + T.c0, T.n)

    def XFBr(T):
        a = 1024 + T.c0 // 2
        n2 = T.n // 2
        return (TMP.t[:, a:a + n2].bitcast(BF16), TMP.keys(a, n2))

    def mixer(l, g, tiles):
        last = (g == n_groups - 1)
        SIGA, AGL, SA, YAG, GEL, BXR, SIGB = 0, 8, 16, 8, 0, 16, 16
        PTs = [T for T in tiles if T.kind == 'P']
        ST = [T for T in tiles if T.kind == 'S'][0]
        if not P.planning:
            rmsnorm_U(l, 2, tiles)
            sq0 = g * SQG
            P.dma_group('pool', s_sa, [(ags(c)[0][:, :, 0:30], sca[l][:, c, sq0:sq0 + SQG, :]) for c in range(NCH)],
                        writes=[AGS.f(0, NCH * SQG * 38)])
            P.dma_group('pool', s_sb, [(bxs(c)[0][:, :, 0:3], scb[l][:, c, sq0:sq0 + SQG, :]) for c in range(NCH)],
                        writes=[BXS.f(0, NCH * SQG * 11)])
            P.dma('sp', s_sh, H0B.t[:, :].rearrange("p (c s) -> p c s", s=SQG), slh[l][:, :, sq0:sq0 + SQG],
                  writes=[H0B.f(0, NCH * 8)])
            P.dma_group('pool', s_lrw, [
                (LRW.t[:, 0:NCH * 128].rearrange("p (n j) -> p n j", j=128), lru_w_a[l].rearrange("n i j -> i n j")),
                (LRW.t[:, NCH * 128:2 * NCH * 128].rearrange("p (n j) -> p n j", j=128), lru_w_x[l].rearrange("n i j -> i n j"))],
                writes=[LRW.f(0, 2 * NCH * 128)])
            if g == 0:
                P.dma('sp', s_cp, ca_s[l][:, :, :, 0:22], sca[l][:, :, :, 8:30])
            for c in range(NCH):
                cp('dve', B.r(AGL + c, 0, 30), CARA.r(l * NCH + c, 0, 30))
        for s_ in range(4):
            W1 = ws.get(w_in, l, s_ * 256, 256, NCH)
            W2 = ws.get(w_in, l, D + s_ * 256, 256, NCH)
            if P.planning:
                continue
            for mi in range(2):
                m = s_ * 2 + mi
                for T in tiles:
                    c0, n = T.c0, T.n
                    p1 = bank(); p2 = bank()
                    p1n = sl(p1, slice(None), slice(0, n)); p2n = sl(p2, slice(None), slice(0, n))
                    for k in range(NCH):
                        mm(p1n, wv(W1, k, mi * 128, 128), U.r(k, c0, n), k == 0, k == NCH - 1)
                    for k in range(NCH):
                        mm(p2n, wv(W2, k, mi * 128, 128), U.r(k, c0, n), k == 0, k == NCH - 1)
                    t = tmp(n)
                    act(t, p2n, AF.Sigmoid, bias=cst(m, R_BIN(l, 1)))
                    b1 = cst(m, R_BIN(l, 0))
                    if T.kind == 'P':
                        stt('dve', B.r(AGL + m, BPAD + c0, n), p1n, b1, t, ALU.add, ALU.mult)
                        if last and T is PTs[-1]:
                            stt('dve', AGF.r(m, SG, 30), sl(p1n, slice(None), slice(n - 30, n)), b1,
                                sl(t, slice(None), slice(n - 30, n)), ALU.add, ALU.mult)
                    else:
                        a3 = ags(m)
                        stt('dve', (a3[0][:, :, 30:38], a3[1]), v3(p1n), b1, v3(t), ALU.add, ALU.mult)
                        stt('dve', AGF.r(m, 0, SG), p1n, b1, t, ALU.add, ALU.mult)
        ga_gen = proj_w_in(l, 4, tiles)
        if P.planning:
            for _ in ga_gen:
                pass

        def ga_slabs(k):
            for _ in range(k * 2 * len(tiles)):
                m_, T_, psn_ = next(ga_gen)
                act(B.r(SIGA + m_, BPAD + T_.c0, T_.n), psn_, AF.Sigmoid, bias=cst(m_, R_BIN(l, 4)))
        if not P.planning:
            for c in range(NCH):
                cp('act', CARA.r(l * NCH + c, 0, 30), B.r(AGL + c, PG, 30))
            agf3 = AGF.t[:, :].rearrange("p (c x) -> p c x", x=SG + 30)
            P.dma_group('sp', s_o1, [(ca_s[l][:, c, g * SQG:(g + 1) * SQG, 22:30],
                                      agf3[:, c, 0:SG].rearrange("p (s t) -> p s t", t=DS)) for c in range(NCH)],
                        reads=[AGF.f(0, NCH * (SG + 30))])
            if last:
                P.dma('sp', s_o1, ca_p[l], agf3[:, :, SG:SG + 30], reads=[AGF.f(0, NCH * (SG + 30))])
            for c in range(NCH):
                a0 = c * NROW + R_CAW(l, 0)
                halves = []
                for (w0, w1) in ((0, 16), (16, 31)):
                    nw = w1 - w0
                    dgh = DIAGA.g(w0 * 128, nw * 128, "p (w j) -> p w j", j=128)
                    P.op('dve', (lambda e, o=dgh[0], i0=IDB.t[:, 0:128].unsqueeze(1).to_broadcast([128, nw, 128]),
                                 i1=CONST.t[:, a0 + w0:a0 + w1].unsqueeze(2).to_broadcast([128, nw, 128]):
                                 e.tensor_tensor(out=o, in0=i0, in1=i1, op=ALU.mult)),
                         reads=[IDB.f(0, 128), CONST.f(0, 1)], writes=[dgh])
                    halves.append(dgh)
                pcs = [bank() for _ in tiles]
                for w in range(31):
                    dgw = DIAGA.f(w * 128, 128)
                    for ti, T in enumerate(tiles):
                        c0, n = T.c0, T.n
                        pcn = sl(pcs[ti], slice(None), slice(0, n))
                        if T.kind == 'P':
                            rhs = B.r(AGL + c, c0 + w, n)
                        else:
                            a3 = ags(c)
                            rhs = (a3[0][:, :, w:w + DS], a3[1])
                        mm(pcn, dgw, rhs, w == 0, w == 30, inc=(w == 30 or w == 15))
                for ti, T in enumerate(tiles):
                    act(Yr(c, T.c0, T.n), sl(pcs[ti], slice(None), slice(0, T.n)), AF.Identity, bias=cst(c, R_CAB(l)))
            for T in tiles:
                c0, n = T.c0, T.n
                pm = bank(); pq = bank()
                pmn = sl(pm, slice(None), slice(0, n)); pqn = sl(pq, slice(None), slice(0, n))
                for c in range(NCH):
                    sq = sqt(n)
                    act(sq, Yr(c, c0, n), AF.Square)
                    mm(pqn, ONES.f(0, 128), sq, c == 0, c == NCH - 1, inc=True)
                    cb = sqt(n)
                    cp('act', cb, Yr(c, c0, n))
                    mm(pmn, ONES.f(0, 128), cb, c == 0, c == NCH - 1, inc=True)
                mean = tmp(n)
                cp('act', mean, pmn)
                m2 = tmp(n)
                tt('dve', m2, mean, mean, ALU.mult)
                tt('dve', m2, pqn, m2, ALU.subtract)
                rs = rst(n)
                act(rs, m2, AF.Ln, bias=der(D_EPS))
                act(rs, rs, AF.Exp, scale=-0.5)
                nmr = rst(n)
                stt('dve', nmr, mean, -1.0, rs, ALU.mult, ALU.mult)
                for c in range(NCH):
                    t = tmp(n)
                    tt('dve', t, Yr(c, c0, n), rs, ALU.mult)
                    tt('dve', t, t, nmr, ALU.add)
                    act(B.r(SA + c, BPAD + c0, n), t, AF.Silu, scale=cst(c, R_LNG(l)), bias=cst(c, R_LNB(l)))
                ga_slabs(2 if T.idx == 1 else 1)
        for m, T, pyn in proj_out(w_a_out, l, lambda k, T: B.r(SA + k, BPAD + T.c0, T.n), tiles):
            tt('dve', B.r(YAG + m, BPAD + T.c0, T.n), pyn, B.r(SIGA + m, BPAD + T.c0, T.n), ALU.mult)
        for m, T, psn in proj_w_in(l, 3, tiles):
            act(B.r(GEL + m, BPAD + T.c0, T.n), psn, AF.Gelu_apprx_tanh, bias=cst(m, R_BIN(l, 3)))
        if not P.planning:
            for c in range(NCH):
                cp('dve', B.r(BXR + c, 27, 3), CARB.r(l * NCH + c, 0, 3))
        for m, T, psn in proj_w_in(l, 2, tiles):
            bb = cst(m, R_BIN(l, 2))
            n = T.n
            if T.kind == 'P':
                act(B.r(BXR + m, BPAD + T.c0, n), psn, AF.Identity, bias=bb)
                if last and T is PTs[-1]:
                    act(BXF.r(m, SG, 3), sl(psn, slice(None), slice(n - 3, n)), AF.Identity, bias=bb)
            else:
                b3 = bxs(m)
                act((b3[0][:, :, 3:11], b3[1]), v3(psn), AF.Identity, bias=bb)
                act(BXF.r(m, 0, SG), psn, AF.Identity, bias=bb)
        if not P.planning:
            for c in range(NCH):
                cp('act', CARB.r(l * NCH + c, 0, 3), B.r(BXR + c, 27 + PG, 3))
            bxf3 = BXF.t[:, :].rearrange("p (c x) -> p c x", x=SG + 3)
            P.dma_group('sp', s_o2, [(cb_s[l][:, c, g * SQG:(g + 1) * SQG, :],
                                      bxf3[:, c, 0:SG].rearrange("p (s t) -> p s t", t=DS)[:, :, 5:8]) for c in range(NCH)],
                        reads=[BXF.f(0, NCH * (SG + 3))])
            if last:
                P.dma('sp', s_o2, cb_p[l], bxf3[:, :, SG:SG + 3], reads=[BXF.f(0, NCH * (SG + 3))])
            SL_A2M_ = 0

            def slots(c):
                p = c % 2
                return 1 + p, 3 + p, 5 + p

            conv_banks = {}

            def lru_P(c):
                dgb = DIAGB.g(0, 4 * 128, "p (w j) -> p w j", j=128)
                a0 = c * NROW + R_CBW(l, 0)
                P.op('dve', (lambda e, o=dgb[0], i0=IDB.t[:, 0:128].unsqueeze(1).to_broadcast([128, 4, 128]),
                             i1=CONST.t[:, a0:a0 + 4].unsqueeze(2).to_broadcast([128, 4, 128]):
                             e.tensor_tensor(out=o, in0=i0, in1=i1, op=ALU.mult)),
                     reads=[IDB.f(0, 128), CONST.f(0, 1)], writes=[dgb])
                bs = []
                for T in tiles:
                    c0, n = T.c0, T.n
                    pc = bank()
                    pcn = sl(pc, slice(None), slice(0, n))
                    for w in range(4):
                        if T.kind == 'P':
                            rhs = B.r(BXR + c, 27 + c0 + w, n)
                        else:
                            b3 = bxs(c)
                            rhs = (b3[0][:, :, w:w + DS], b3[1])
                        mm(pcn, (dgb[0][:, w, :], dgb[1]), rhs, w == 0, w == 3)
                    bs.append(pcn)
                conv_banks[c] = bs

            def lru_A(c):
                SL_XH, SL_RA, SL_IU = slots(c)
                for ti, T in enumerate(tiles):
                    ts('dve', Sf(SL_XH, T), conv_banks[c][ti], cst(c, R_CBB(l)), ALU.add)
                cp('dve', (TMP.t[:, 1024:1024 + TOK // 2].bitcast(BF16), TMP.keys(1024, TOK // 2)), S.f(SL_XH * TOK, TOK))

            lru_banks = {}

            def lru_C(c):
                bs = []
                for T in tiles:
                    n = T.n
                    pr = bank(); pi = bank()
                    prn = sl(pr, slice(None), slice(0, n)); pin = sl(pi, slice(None), slice(0, n))
                    mm(pin, LRW.f(NCH * 128 + c * 128, 128), XFBr(T), True, True)
                    mm(prn, LRW.f(c * 128, 128), XFBr(T), True, True)
                    bs.append((prn, pin))
                lru_banks[c] = bs

            def lru_D(c):
                SL_XH, SL_RA, SL_IU = slots(c)
                for ti, T in enumerate(tiles):
                    prn, pin = lru_banks[c][ti]
                    act(Sf(SL_IU, T), pin, AF.Sigmoid, bias=cst(c, R_LBX(l)))
                for ti, T in enumerate(tiles):
                    prn, pin = lru_banks[c][ti]
                    act(Sf(SL_RA, T), prn, AF.Sigmoid, bias=cst(c, R_LBA(l)))

            TALL = Tile(9, 0, TOK, 'A')
            TPR = Tile(8, 0, PG, 'P')

            def lru_E(c):
                SL_XH, SL_RA, SL_IU = slots(c)
                tt('dve', Sf(SL_IU, TALL), Sf(SL_IU, TALL), Sf(SL_XH, TALL), ALU.mult)

            def lru_Fa(c):
                SL_XH, SL_RA, SL_IU = slots(c)
                act(Sf(SL_RA, TALL), Sf(SL_RA, TALL), AF.Exp, scale=der(D_CL(l, c)))

            def lru_Fb(c):
                SL_XH, SL_RA, SL_IU = slots(c)
                tt('dve', Sf(SL_A2M_, TALL), Sf(SL_RA, TALL), Sf(SL_RA, TALL), ALU.mult)

            def lru_Fc(c):
                act(Sf(SL_A2M_, TALL), Sf(SL_A2M_, TALL), AF.Sqrt, scale=-1.0, bias=der(D_ONE))

            def lru_G(c):
                SL_XH, SL_RA, SL_IU = slots(c)
                tt('dve', Sf(SL_IU, TALL), Sf(SL_IU, TALL), Sf(SL_A2M_, TALL), ALU.mult)
                hc = CARH.f(l * NCH + c, 1)
                scan(Sf(SL_XH, TPR), Sf(SL_RA, TPR), Sf(SL_IU, TPR), hc)
                cp('dve', hc, sl(Sf(SL_XH, TPR), slice(None), slice(PG - 1, PG)))
                if last:
                    cp('act', HLP.f(c, 1), sl(Sf(SL_XH, TPR), slice(None), slice(PG - 1, PG)))
                T = ST
                A3 = v3(Sf(SL_RA, T)); U3 = v3(Sf(SL_IU, T)); H3 = v3(Sf(SL_XH, T))
                t8 = SMALL.f(0, 8)
                a30 = (A3[0][:, :, 0], A3[1]); u30 = (U3[0][:, :, 0], U3[1])
                tt('dve', t8, a30, H0B.r(c, 0, SQG), ALU.mult)
                tt('dve', u30, u30, t8, ALU.add)
                memset('dve', a30, 0.0)
                scan(Sf(SL_XH, T), Sf(SL_RA, T), Sf(SL_IU, T), 0.0)
                cp('act', HLB.r(c, 0, SQG), (H3[0][:, :, DS - 1], H3[1]))
                hg = B.r(GEL + c, BPAD, TOK)
                tt('dve', hg, Sf(SL_XH, TALL), hg, ALU.mult)

            lru_P(0); lru_A(0); lru_C(0)
            for c in range(NCH):
                lru_D(c)
                if c + 1 < NCH:
                    lru_P(c + 1)
                lru_E(c)
                lru_Fa(c)
                lru_Fb(c)
                if c + 1 < NCH:
                    lru_A(c + 1)
                    lru_C(c + 1)
                lru_Fc(c)
                lru_G(c)
            hl3 = HLB.t[:, :].rearrange("p (c x) -> p c x", x=16)
            P.dma('sp', s_o3, lh_s[l][:, :, g * SQG:(g + 1) * SQG], hl3[:, :, 0:SQG], reads=[HLB.f(0, NCH * 16)])
            if last:
                P.dma('sp', s_o3, lh_p[l], HLP.t[:, 0:NCH], reads=[HLP.f(0, NCH)])
        for m, T, psn in proj_w_in(l, 5, tiles):
            act(B.r(SIGB + m, BPAD + T.c0, T.n), psn, AF.Sigmoid, bias=cst(m, R_BIN(l, 5)))
        for m, T, pyn in proj_out(w_b_out, l, lambda k, T: B.r(GEL + k, BPAD + T.c0, T.n), tiles):
            t = tmp(T.n)
            tt('dve', t, pyn, B.r(SIGB + m, BPAD + T.c0, T.n), ALU.mult)
            yg = B.r(YAG + m, BPAD + T.c0, T.n)
            tt('dve', yg, t, yg, ALU.add)
        stats = reserve(3)
        for m, T, pyn in proj_out(w_out, l, lambda k, T: B.r(YAG + k, BPAD + T.c0, T.n), tiles):
            evac_y(m, T, pyn, stats)
        if not P.planning:
            post_residual(l, lambda c: cst(c, R_NG(l, 3)), tiles, stats)
        release(stats)


    MASK = Buf(nc, "MASK", 1, SQG * SG, BF16, page=SQG * SG)
    s_mem = P.slot("s_mem"); s_kt = [P.slot("s_kt0"), P.slot("s_kt1")]; s_vv = [P.slot("s_v0"), P.slot("s_v1")]
    s_ok = P.slot("s_ok")
    out_slots.append(s_ok)
    XS0 = 16 * BCOLS

    def Sb(a, n, pattern=None, parts=128, **kw):
        ap = S.t[0:parts, a:a + n].bitcast(BF16)
        if pattern is not None:
            ap = ap.rearrange(pattern, **kw)
        return (ap, S.keys(a, n))

    def bankbf(parts=128):
        b = bank()
        return (b[0][0:parts, :].bitcast(BF16), b[1])

    def init_mask():
        memset('dve', MASK.f(0, SQG * SG), 0.0)
        for b in range(SQG):
            memset('dve', MASK.f(b * SG + b * DS, DS), 1.0)

    SM2 = Buf(nc, "SM2", 1, 64, F32, page=1)
    SM3 = Buf(nc, "SM3", 1, 32, F32, page=1)

    def softmax_rows(ps, pm_out, ssum, parts, h=0):
        mx = SM3.f(h, 1, parts)
        nmx = SM3.f(16 + h, 1, parts)
        P.op('dve', (lambda e, o=mx[0], i=ps[0]: e.reduce_max(out=o, in_=i, axis=AX.X)), reads=[ps], writes=[mx])
        ts('dve', nmx, mx, -1.0 / 16.0, ALU.mult)
        act(pm_out, ps, AF.Exp, bias=nmx, scale=1.0 / 16.0, accum=ssum)

    def xattn(l, g, tiles):
        Q, OT = 0, 8
        PTs = [T for T in tiles if T.kind == 'P']
        ST = [T for T in tiles if T.kind == 'S'][0]
        KTP = B.g(XS0, 2048, "p (j m) -> p j m", m=NMEM)
        VP = B.g(XS0 + 2048, 2048, "p (c f) -> p c f", f=D)
        MT = B.g(XS0 + 4096, 2048, "p (c m) -> p c m", m=NMEM)
        PTb = B.f(XS0 + 6144, 256)
        PM = lambda parts: B.f(XS0 + 6400, 1024, parts)
        OTM = lambda parts: B.f(XS0 + 7424, 1024, parts)
        MEMF = S.g(0, NCH * NMEM, "p (c m) -> p c m", m=NMEM)
        KST = S.g(2048, 2048, "p (c f) -> p c f", f=D)
        if not P.planning:
            rmsnorm_U(l, 4, tiles)
            if 'mem' not in XA_SKIP:
                P.dma('sp', s_mem, MEMF[0], memT, writes=[MEMF])
            ms = bank()
            msn = sl(ms, slice(None), slice(0, NMEM))
            for c in range(NCH):
                sq = sqt(NMEM)
                act(sq, (MEMF[0][:, c, :], MEMF[1]), AF.Square)
                mm(msn, ONES.f(0, 128), sq, c == 0, c == NCH - 1, inc=True)
            rs = rst(NMEM)
            act(rs, msn, AF.Ln, bias=der(D_EPS))
            act(rs, rs, AF.Exp, scale=-0.5)
            for c in range(NCH):
                stt('dve', (MT[0][:, c, :], MT[1]), (MEMF[0][:, c, :], MEMF[1]), cst(c, R_NG(l, 6)), rs, ALU.mult, ALU.mult)
        for which, wap in ((0, xa_w_k), (1, xa_w_v)):
            for s_ in range(4):
                W = ws.get(wap, l, s_ * 256, 256, NCH)
                if P.planning or 'kv' in XA_SKIP:
                    continue
                if which == 0:
                    for mi in range(2):
                        j = s_ * 2 + mi
                        pk = bank()
                        pkn = sl(pk, slice(None), slice(0, NMEM))
                        for k in range(NCH):
                            mm(pkn, wv(W, k, mi * 128, 128), (MT[0][:, k, :], MT[1]), k == 0, k == NCH - 1)
                        cp('act', (KTP[0][:, j, :], KTP[1]), pkn)
                for mc in range(2):
                    if which == 0 and g != 0:
                        continue
                    pk = bank()
                    pkn = sl(pk, slice(None), slice(0, 256))
                    for k in range(NCH):
                        mm(pkn, (MT[0][:, k, mc * 128:(mc + 1) * 128], MT[1]), (W[0][:, k, :], W[1]), k == 0, k == NCH - 1)
                    if which == 1:
                        cp('act', (VP[0][:, mc, s_ * 256:(s_ + 1) * 256], VP[1]), pkn)
                    if g == 0:
                        cp('dve', (KST[0][:, mc, s_ * 256:(s_ + 1) * 256], KST[1]), pkn)
            if not P.planning and g == 0 and 'kvout' not in XA_SKIP and 'kv' not in XA_SKIP:
                dst = (mk_o if which == 0 else mv_o)[l].rearrange("(c p) f -> p c f", p=128)
                P.dma('sp', s_ok, dst, KST[0], reads=[KST])
        for m, T, pyn in proj_out(xa_w_q, l, lambda k, T: U.r(k, T.c0, T.n), tiles):
            cp('act', B.r(Q + m, BPAD + T.c0, T.n), pyn)
        if not P.planning:
            PTb8 = B.f(XS0 + 4096, 2048)
            PMb = [B.f(XS0 + 6144, 2048), Sb(0, 1024)]
            hb = [(bi, h) for bi in range(2) for h in range(4)]
            NPAIR = 0 if 'pattn' in XA_SKIP else PG // 256
            att = {}

            def att_qk(p):
                cols = [BPAD + (p * 2 + bi) * 128 for bi in range(2)]
                pss = [bank() for _ in range(4)]
                for i, (bi, h) in enumerate(hb):
                    psn = sl(pss[i // 2], slice(None), slice((i % 2) * 256, (i % 2) * 256 + 256))
                    for q2 in range(2):
                        mm(psn, B.r(Q + h * 2 + q2, cols[bi], 128), (KTP[0][:, h * 2 + q2, :], KTP[1]), q2 == 0, q2 == 1)
                att[p] = pss

            def att_softmax(p):
                pss = att[p]
                PM2 = PMb[p % 2]
                o3 = (p % 2) * 8
                mx8 = SM3.f(o3, 8)
                nmx8 = SM3.f(16 + o3, 8)
                for b_ in range(4):
                    src = (pss[b_][0][:, :].rearrange("p (a m) -> p a m", m=NMEM), pss[b_][1])
                    dst = SM3.f(o3 + 2 * b_, 2)
                    P.op('dve', (lambda e, o=dst[0], i=src[0]: e.reduce_max(out=o, in_=i, axis=AX.X)), reads=[src], writes=[dst])
                ts('dve', nmx8, mx8, -1.0 / 16.0, ALU.mult)
                for i, (bi, h) in enumerate(hb):
                    psn = sl(pss[i // 2], slice(None), slice((i % 2) * 256, (i % 2) * 256 + 256))
                    act(sl(PM2, slice(None), slice(i * 256, (i + 1) * 256)), psn, AF.Exp, bias=SM3.f(16 + o3 + i, 1),
                        scale=1.0 / 16.0, accum=SM2.f((p % 2) * 16 + i, 1))

            def att_transP(p):
                PM2 = PMb[p % 2]
                ptps = [bankbf(), bankbf()]
                for i in range(8):
                    for mc in range(2):
                        a = (i % 4) * 256 + mc * 128
                        transpose(sl(ptps[i // 4], slice(None), slice(a, a + 128)),
                                  sl(PM2, slice(None), slice(i * 256 + mc * 128, i * 256 + mc * 128 + 128)), IDB.f(0, 128))
                cp('dve', sl(PTb8, slice(None), slice(0, 1024)), ptps[0])
                cp('act', sl(PTb8, slice(None), slice(1024, 2048)), ptps[1])

            def att_pv(p):
                pos = [bank() for _ in range(4)]
                for i, (bi, h) in enumerate(hb):
                    pon = sl(pos[i // 2], slice(None), slice((i % 2) * 256, (i % 2) * 256 + 256))
                    for mc in range(2):
                        a = i * 256 + mc * 128
                        mm(pon, sl(PTb8, slice(None), slice(a, a + 128)),
                           (VP[0][:, mc, h * 256:(h + 1) * 256], VP[1]), mc == 0, mc == 1)
                att[('o', p)] = pos

            def att_out(p):
                PM2 = PMb[p % 2]
                pos = att[('o', p)]
                cols = [BPAD + (p * 2 + bi) * 128 for bi in range(2)]
                o0 = (p % 2) * 16
                rs8 = SM2.f(o0 + 8, 8)
                recip(rs8, SM2.f(o0, 8), fast=False)
                for i, (bi, h) in enumerate(hb):
                    pon = sl(pos[i // 2], slice(None), slice((i % 2) * 256, (i % 2) * 256 + 256))
                    dst = sl(PM2, slice(None), slice(i * 256, (i + 1) * 256))
                    if i % 2 == 0:
                        act(dst, pon, AF.Copy, scale=SM2.f(o0 + 8 + i, 1))
                    else:
                        ts('dve', dst, pon, SM2.f(o0 + 8 + i, 1), ALU.mult)
                otps = [bankbf(), bankbf()]
                for bi in range(2):
                    for j in range(NCH):
                        transpose(sl(otps[bi], slice(None), slice(j * 128, (j + 1) * 128)),
                                  sl(PM2, slice(None), slice(bi * 1024 + j * 128, bi * 1024 + (j + 1) * 128)), IDB.f(0, 128))
                for bi in range(2):
                    col = cols[bi]
                    for half in range(2):
                        j0 = half * 4
                        dst_ap = B.t[:, (OT + j0) * BCOLS:(OT + j0 + 4) * BCOLS].rearrange("p (r c) -> p r c", c=BCOLS)[:, :, col:col + 128]
                        dst_k = []
                        for j in range(j0, j0 + 4):
                            dst_k += B.keys((OT + j) * BCOLS + col, 128)
                        src = (otps[bi][0][:, j0 * 128:(j0 + 4) * 128].rearrange("p (r c) -> p r c", c=128), otps[bi][1])
                        cp('act' if half == 0 else 'dve', (dst_ap, dst_k), src)

            if NPAIR:
                att_qk(0)
                att_softmax(0)
            for p in range(NPAIR):
                if p + 1 < NPAIR:
                    att_qk(p + 1)
                att_transP(p)
                if p + 1 < NPAIR:
                    att_softmax(p + 1)
                att_pv(p)
                att_out(p)
            if 'sattn' not in XA_SKIP:
                QPAD = Sb(0, 2048, "p (j b t) -> p j b t", b=SQG, t=SG)
                PTPAD = Sb(2048, 2048, "p (h c b t) -> p h c b t", c=2, b=SQG, t=SG)
                KTs = [Sb(4096 + i * 1024, 1024, "p (j m) -> p j m", m=NMEM) for i in range(2)]
                Vs = [Sb(6144 + i * 1024, 1024, "p (c f) -> p c f", f=D) for i in range(2)]
                mask3 = MASK.g(0, SQG * SG, "p (b t) -> p b t", t=SG)
                scol = BPAD + ST.c0
                for j in range(NCH):
                    qs = B.r(Q + j, scol, SG)
                    tt('dve', (QPAD[0][:, j, :, :], QPAD[1]), (qs[0].unsqueeze(1).to_broadcast([128, SQG, SG]), qs[1]), mask3, ALU.mult)
                psh = [bank() for _ in range(4)]
                sq0 = g * SQG
                for b in range(SQG):
                    kt = KTs[b % 2]
                    P.dma('pool', s_kt[b % 2], kt[0], kcT[l, sq0 + b], writes=[kt])
                    for h in range(4):
                        for q2 in range(2):
                            mm((psh[h][0][0:SG, 0:NMEM], psh[h][1]), (QPAD[0][:, h * 2 + q2, b, :], QPAD[1]),
                               (kt[0][:, h * 2 + q2, :], kt[1]), b == 0 and q2 == 0, b == SQG - 1 and q2 == 1,
                               inc=(q2 == 1))
                for h in range(4):
                    ssum = SM2.f(h, 1, SG)
                    pmh = sl(PM(SG), slice(None), slice(h * 256, (h + 1) * 256))
                    softmax_rows((psh[h][0][0:SG, 0:NMEM], psh[h][1]), pmh, ssum, SG, h)
                    ptp = bankbf()
                    for mc in range(2):
                        transpose(sl(ptp, slice(None), slice(mc * SG, (mc + 1) * SG)),
                                  sl(pmh, slice(None), slice(mc * 128, (mc + 1) * 128)), (IDB.t[0:SG, 0:SG], IDB.keys(0, 128)))
                    for mc in range(2):
                        src = sl(ptp, slice(None), slice(mc * SG, (mc + 1) * SG))
                        tt('dve', (PTPAD[0][:, h, mc, :, :], PTPAD[1]),
                           (src[0].unsqueeze(1).to_broadcast([128, SQG, SG]), src[1]), mask3, ALU.mult)
                poh = [bank() for _ in range(4)]
                for b in range(SQG):
                    vb = Vs[b % 2]
                    P.dma('pool', s_vv[b % 2], vb[0], vc[l, sq0 + b], writes=[vb])
                    for h in range(4):
                        for mc in range(2):
                            mm((poh[h][0][0:SG, 0:256], poh[h][1]), (PTPAD[0][:, h, mc, b, :], PTPAD[1]),
                               (vb[0][:, mc, h * 256:(h + 1) * 256], vb[1]), b == 0 and mc == 0, b == SQG - 1 and mc == 1,
                               inc=(mc == 1))
                for h in range(4):
                    ssum = SM2.f(h, 1, SG)
                    rsm = SM2.f(8 + h, 1, SG)
                    recip(rsm, ssum, fast=False)
                    act(sl(OTM(SG), slice(None), slice(h * 256, (h + 1) * 256)), (poh[h][0][0:SG, 0:256], poh[h][1]),
                        AF.Copy, scale=rsm)
                otp = bankbf()
                for j in range(NCH):
                    transpose(sl(otp, slice(None), slice(j * SG, (j + 1) * SG)),
                              sl(OTM(SG), slice(None), slice(j * 128, (j + 1) * 128)), (IDB.t[0:SG, 0:SG], IDB.keys(0, 128)))
                for j in range(NCH):
                    cp('dve' if j % 2 else 'act', B.r(OT + j, scol, SG), sl(otp, slice(None), slice(j * SG, (j + 1) * SG)))
        stats = reserve(3)
        for m, T, pyn in proj_out(xa_w_o, l, lambda k, T: B.r(OT + k, BPAD + T.c0, T.n), tiles):
            evac_y(m, T, pyn, stats)
        if not P.planning:
            post_residual(l, lambda c: cst(c, R_NG(l, 5)), tiles, stats)
        release(stats)

    def emit_all():
        pstate['i'] = 0
        if not P.planning:
            init_consts()
            init_mask()
        for g in range(n_groups):
            tiles = TILES
            if not P.planning:
                for c in range(NCH):
                    pass
                P.dma('sp', s_x, X.t[:, :].rearrange("p (c t) -> p c t", t=TOK), xin[:, :, g * TOK:(g + 1) * TOK],
                      writes=[X.f(0, NCH * TOK)])
            for l in range(n_layers):
                if 'ffn1' in stages:
                    ffn(l, ffn1_w_in, ffn1_w_out, 0, 0, tiles)
                if 'mix' in stages:
                    mixer(l, g, tiles)
                if 'xa' in stages:
                    xattn(l, g, tiles)
                if 'ffn2' in stages:
                    ffn(l, ffn2_w_in, ffn2_w_out, 7, 1, tiles)
            if not P.planning:
                P.dma('sp', s_y, yout[:, :, g * TOK:(g + 1) * TOK], X.t[:, :].rearrange("p (c t) -> p c t", t=TOK),
                      reads=[X.f(0, NCH * TOK)])

    P.planning = True
    emit_all()
    P.planning = False
    emit_all()
    P.final_wait('sp', out_slots)
    P.emit()
    return nc


_WNAMES = ["w_in", "w_a_out", "w_b_out", "w_out", "lru_w_a", "lru_w_x", "xa_w_q", "xa_w_k", "xa_w_v", "xa_w_o",
           "ffn1_w_in", "ffn1_w_out", "ffn2_w_in", "ffn2_w_out"]


def _fm(a):
    sh = a.shape
    a = a.reshape(sh[:-2] + (sh[-2], NCH, 128))
    nd = a.ndim
    perm = tuple(range(nd - 3)) + (nd - 1, nd - 2, nd - 3)
    return np.ascontiguousarray(a.transpose(perm))


def _unfm(a):
    nd = a.ndim
    perm = tuple(range(nd - 3)) + (nd - 1, nd - 2, nd - 3)
    a = a.transpose(perm)
    return np.ascontiguousarray(a.reshape(a.shape[:-2] + (D,)))


def prepare_inputs(inp):
    f = lambda k: np.asarray(inp[k], dtype=np.float32)
    xp = f('x_prompt')
    xs = f('x_sample')
    mem = f('mem_prompt')
    ck = f('cache_mem_k')
    cv = f('cache_mem_v')
    sa = f('state_conv_a')
    sb = f('state_conv_b')
    sh = f('state_lru_h')
    rows = np.zeros((NROW, D), np.float32)
    ng = f('norm_g')
    for l in range(DEPTH):
        for i in range(9):
            rows[R_NG(l, i)] = ng[l, i]
        for j in range(6):
            rows[R_BIN(l, j)] = f('b_in')[l, j * D:(j + 1) * D]
        for w in range(31):
            rows[R_CAW(l, w)] = f('conv_a_w')[l, w]
        rows[R_CAB(l)] = f('conv_a_b')[l]
        rows[R_LNG(l)] = f('conv_ln_g')[l]
        rows[R_LNB(l)] = f('conv_ln_b')[l]
        for w in range(4):
            rows[R_CBW(l, w)] = f('conv_b_w')[l, w]
        rows[R_CBB(l)] = f('conv_b_b')[l]
        rows[R_LBA(l)] = f('lru_b_a')[l]
        rows[R_LBX(l)] = f('lru_b_x')[l]
        rows[R_LAM(l)] = f('lru_lambda')[l]
    consts = np.ascontiguousarray(rows.reshape(NROW, NCH, 128).transpose(2, 1, 0))
    wts = {}
    for k in _WNAMES:
        a = f(k)
        if k in ("xa_w_q", "xa_w_k", "xa_w_v", "xa_w_o"):
            a = a.reshape(DEPTH, D, D)
        wts[k] = np.ascontiguousarray(a)
    in_maps = []
    for c in range(NCORES):
        toks = []
        for g in range(NGRP):
            toks.append(xp[c, g * PG:(g + 1) * PG])
            s0 = c * NSEQ_S + g * SQG
            toks.append(xs[s0:s0 + SQG].reshape(SG, D))
        xt = np.concatenate(toks, axis=0)
        m = dict(wts)
        m["xin"] = _fm(xt)
        m["memT"] = _fm(mem[c])
        sl_ = slice(c * NSEQ_S, (c + 1) * NSEQ_S)
        k5 = ck[:, sl_]
        k5 = k5.reshape(DEPTH, NSEQ_S, NMEM, 4, 2, 128)
        m["kcT"] = np.ascontiguousarray(k5.transpose(0, 1, 5, 3, 4, 2).reshape(DEPTH, NSEQ_S, 128, 8, NMEM))
        v5 = cv[:, sl_].reshape(DEPTH, NSEQ_S, 2, 128, D)
        m["vc"] = np.ascontiguousarray(v5.transpose(0, 1, 3, 2, 4))
        a5 = sa[:, sl_].reshape(DEPTH, NSEQ_S, 30, NCH, 128)
        m["sca"] = np.ascontiguousarray(a5.transpose(0, 4, 3, 1, 2))
        b5 = sb[:, sl_].reshape(DEPTH, NSEQ_S, 3, NCH, 128)
        m["scb"] = np.ascontiguousarray(b5.transpose(0, 4, 3, 1, 2))
        h4 = sh[:, sl_].reshape(DEPTH, NSEQ_S, NCH, 128)
        m["slh"] = np.ascontiguousarray(h4.transpose(0, 3, 2, 1))
        m["consts"] = consts
        in_maps.append(m)
    return in_maps


def assemble_outputs(results):
    y_prompt = np.zeros((NCORES, SEQ, D), np.float32)
    y_sample = np.zeros((NCORES * NSEQ_S, DS, D), np.float32)
    mem_k = np.zeros((DEPTH, NCORES, NMEM, 4, 256), np.float32)
    mem_v = np.zeros((DEPTH, NCORES, NMEM, 4, 256), np.float32)
    ca_p = np.zeros((DEPTH, NCORES, 30, D), np.float32)
    cb_p = np.zeros((DEPTH, NCORES, 3, D), np.float32)
    lh_p = np.zeros((DEPTH, NCORES, D), np.float32)
    ca_s = np.zeros((DEPTH, NCORES * NSEQ_S, 30, D), np.float32)
    cb_s = np.zeros((DEPTH, NCORES * NSEQ_S, 3, D), np.float32)
    lh_s = np.zeros((DEPTH, NCORES * NSEQ_S, D), np.float32)
    for c in range(NCORES):
        r = results[c]
        yt = _unfm(np.asarray(r["yout"]))
        for g in range(NGRP):
            base = g * TOK
            y_prompt[c, g * PG:(g + 1) * PG] = yt[base:base + PG]
            s0 = c * NSEQ_S + g * SQG
            y_sample[s0:s0 + SQG] = yt[base + PG:base + TOK].reshape(SQG, DS, D)
        mem_k[:, c] = np.asarray(r["mk"]).reshape(DEPTH, NMEM, 4, 256)
        mem_v[:, c] = np.asarray(r["mv"]).reshape(DEPTH, NMEM, 4, 256)
        ca_p[:, c] = _unfm(np.asarray(r["ca_p"]))
        cb_p[:, c] = _unfm(np.asarray(r["cb_p"]))
        lh_p[:, c] = np.asarray(r["lh_p"]).transpose(0, 2, 1).reshape(DEPTH, D)
        sl_ = slice(c * NSEQ_S, (c + 1) * NSEQ_S)
        a = np.asarray(r["ca_s"])
        ca_s[:, sl_] = a.transpose(0, 3, 4, 2, 1).reshape(DEPTH, NSEQ_S, 30, D)
        b = np.asarray(r["cb_s"])
        cb_s[:, sl_] = b.transpose(0, 3, 4, 2, 1).reshape(DEPTH, NSEQ_S, 3, D)
        h = np.asarray(r["lh_s"])
        lh_s[:, sl_] = h.transpose(0, 3, 2, 1).reshape(DEPTH, NSEQ_S, D)
    return (y_prompt, y_sample, mem_k, mem_v, ca_p, cb_p, lh_p, ca_s, cb_s, lh_s)


def kernel(**inputs):
    in_maps = prepare_inputs(inputs)
    nc = build_program()
    res = run_bass_kernel_spmd(nc, in_maps, core_ids=list(range(NCORES)))
    return assemble_outputs(res.results)
```

```python
import numpy as np
import concourse.bass as bass
import concourse.mybir as mybir
from concourse.bass_utils import run_bass_kernel_spmd

F32 = mybir.dt.float32
BF16 = mybir.dt.bfloat16
AF = mybir.ActivationFunctionType
ALU = mybir.AluOpType
AX = mybir.AxisListType

D = 1024
NCH = 8
DFF = 2816
NFF = 22
DIN = 6144
SEQ = 2048
NSEQ_S = 16
DS = 8
NGRP = 2
PG = 1024
SQG = 8
SG = SQG * DS
TOK = PG + SG
BPAD = 30
BCOLS = TOK + BPAD
NMEM = 256
NCORES = 8
EPS = 1e-6
DEPTH = 2
TOKC = NGRP * TOK

def R_NG(l, i): return l * 9 + i
def R_BIN(l, j): return 18 + l * 6 + j
def R_CAW(l, w): return 30 + l * 31 + w
def R_CAB(l): return 92 + l
def R_LNG(l): return 94 + l
def R_LNB(l): return 96 + l
def R_CBW(l, w): return 98 + l * 4 + w
def R_CBB(l): return 106 + l
def R_LBA(l): return 108 + l
def R_LBX(l): return 110 + l
def R_LAM(l): return 112 + l
NROW = 114

import os
XA_SKIP = set(os.environ.get('XA_SKIP', '').split(','))
STAGES = ('ffn1', 'mix', 'xa', 'ffn2')
N_LAYERS = DEPTH
NSLOT = 4
SLOT_ELEMS = NFF * 128


def _keys(regs):
    out = []
    for r in regs:
        if isinstance(r, tuple) and len(r) == 2 and isinstance(r[1], list):
            out.extend(r[1])
        elif isinstance(r, list):
            out.extend(r)
        else:
            out.append(r)
    return out


class Slot:
    def __init__(self, sem):
        self.sem = sem
        self.count = 0


class Prog:
    def __init__(self, nc):
        self.nc = nc
        self.planning = False
        self.csem = {e: nc.alloc_semaphore('c_' + e) for e in ('pe', 'act', 'dve', 'pool')}
        self.cnt = {e: 0 for e in self.csem}
        self.ops = {e: [] for e in ('pe', 'act', 'dve', 'pool', 'sp')}
        self.seen = {e: {} for e in self.ops}
        self.lw = {}
        self.rd = {}
        self.nsem = 0

    def slot(self, name):
        return Slot(self.nc.alloc_semaphore(name))

    def _waits(self, eng, rk, wk):
        deps = {}

        def add(t):
            if t is None:
                return
            n = t[0].name
            if n not in deps or deps[n][1] < t[1]:
                deps[n] = t
        for k in rk:
            add(self.lw.get(k))
        for k in wk:
            add(self.lw.get(k))
            d = self.rd.get(k)
            if d:
                for t in d.values():
                    add(t)
        out = []
        own = self.csem.get(eng)
        for n, (sem, v) in deps.items():
            if eng == 'pe' and sem is own:
                continue
            if self.seen[eng].get(n, 0) >= v:
                continue
            self.seen[eng][n] = v
            out.append((sem, v))
        return out

    def _reg(self, tok, rk, wk):
        for k in wk:
            self.lw[k] = tok
            self.rd[k] = {}
        n = tok[0].name
        for k in rk:
            d = self.rd.setdefault(k, {})
            if n not in d or d[n][1] < tok[1]:
                d[n] = tok

    def op(self, eng, fn, reads=(), writes=(), inc=True):
        if self.planning:
            return
        rk = _keys(reads)
        wk = _keys(writes)
        psr = [k for k in rk if k[0] == 'ps']
        if psr:
            wk = wk + psr
            rk = [k for k in rk if k[0] != 'ps']
        waits = self._waits(eng, rk, wk)
        sem = self.csem[eng]
        if inc:
            self.cnt[eng] += 1
            tok = (sem, self.cnt[eng])
        else:
            tok = (sem, self.cnt[eng] + 1)
        self.ops[eng].append((waits, fn, (sem, 1) if inc else None))
        self._reg(tok, rk, wk)

    def dma(self, q, slot, out_ap, in_ap, reads=(), writes=()):
        if self.planning:
            return
        rk = _keys(reads)
        wk = _keys(writes)
        waits = self._waits(q, rk, wk)
        slot.count += 16
        tok = (slot.sem, slot.count)
        self.ops[q].append((waits, (lambda e, o=out_ap, i=in_ap: e.dma_start(out=o, in_=i)), (slot.sem, 16)))
        self._reg(tok, rk, wk)

    def dma_group(self, q, slot, pairs, reads=(), writes=()):
        if self.planning:
            return
        rk = _keys(reads)
        wk = _keys(writes)
        waits = self._waits(q, rk, wk)
        for j, (o_, i_) in enumerate(pairs):
            slot.count += 16
            self.ops[q].append((waits if j == 0 else [], (lambda e, o=o_, i=i_: e.dma_start(out=o, in_=i)), (slot.sem, 16)))
        tok = (slot.sem, slot.count)
        self._reg(tok, rk, wk)

    def final_wait(self, q, slots):
        waits = [(s.sem, s.count) for s in slots if s.count > 0]
        self.ops[q].append((waits, None, None))

    def emit(self):
        nc = self.nc
        ops = self.ops

        def mk(name):
            def body(e):
                for waits, fn, inc in ops[name]:
                    for sem, v in waits:
                        e.wait_ge(sem, v)
                    if fn is None:
                        continue
                    ins = fn(e)
                    if inc is not None:
                        ins.then_inc(inc[0], inc[1])
            return body
        with nc.Block() as block:
            block.tensor(mk('pe'))
            block.scalar(mk('act'))
            block.vector(mk('dve'))
            block.gpsimd(mk('pool'))
            block.sync(mk('sp'))


class Buf:
    def __init__(self, nc, name, rows, cols, dt, page=64):
        self.name = name
        self.rows = rows
        self.cols = cols
        self.page = page
        self.dt = dt
        self.t = nc.alloc_sbuf_tensor(name, [128, rows * cols], dt)

    def keys(self, a, n):
        return [(self.name, p) for p in range(a // self.page, (a + n - 1) // self.page + 1)]

    def f(self, a, n, parts=128):
        return (self.t[0:parts, a:a + n], self.keys(a, n))

    def r(self, row, c0, n, parts=128):
        return self.f(row * self.cols + c0, n, parts)

    def g(self, a, n, pattern, parts=128, **kw):
        return (self.t[0:parts, a:a + n].rearrange(pattern, **kw), self.keys(a, n))


class Tile:
    def __init__(self, idx, c0, n, kind):
        self.idx = idx
        self.c0 = c0
        self.n = n
        self.kind = kind


TILES = [Tile(0, 0, 512, 'P'), Tile(1, 512, 512, 'P'), Tile(2, 1024, SG, 'S')]


def build_program(stages=STAGES, n_layers=N_LAYERS, n_groups=NGRP):
    nc = bass.Bass("TRN2", target_bir_lowering=False)
    P = Prog(nc)

    def dram_in(name, shape):
        return nc.dram_tensor(name, list(shape), F32, kind="ExternalInput").ap()

    def dram_out(name, shape):
        return nc.dram_tensor(name, list(shape), F32, kind="ExternalOutput").ap()

    xin = dram_in("xin", [128, NCH, TOKC])
    memT = dram_in("memT", [128, NCH, NMEM])
    kcT = dram_in("kcT", [DEPTH, NSEQ_S, 128, 8, NMEM])
    vc = dram_in("vc", [DEPTH, NSEQ_S, 128, 2, D])
    sca = dram_in("sca", [DEPTH, 128, NCH, NSEQ_S, 30])
    scb = dram_in("scb", [DEPTH, 128, NCH, NSEQ_S, 3])
    slh = dram_in("slh", [DEPTH, 128, NCH, NSEQ_S])
    consts_d = dram_in("consts", [128, NCH, NROW])
    w_in = dram_in("w_in", [DEPTH, D, DIN])
    w_a_out = dram_in("w_a_out", [DEPTH, D, D])
    w_b_out = dram_in("w_b_out", [DEPTH, D, D])
    w_out = dram_in("w_out", [DEPTH, D, D])
    lru_w_a = dram_in("lru_w_a", [DEPTH, 8, 128, 128])
    lru_w_x = dram_in("lru_w_x", [DEPTH, 8, 128, 128])
    xa_w_q = dram_in("xa_w_q", [DEPTH, D, D])
    xa_w_k = dram_in("xa_w_k", [DEPTH, D, D])
    xa_w_v = dram_in("xa_w_v", [DEPTH, D, D])
    xa_w_o = dram_in("xa_w_o", [DEPTH, D, D])
    ffn1_w_in = dram_in("ffn1_w_in", [DEPTH, D, 2 * DFF])
    ffn1_w_out = dram_in("ffn1_w_out", [DEPTH, DFF, D])
    ffn2_w_in = dram_in("ffn2_w_in", [DEPTH, D, 2 * DFF])
    ffn2_w_out = dram_in("ffn2_w_out", [DEPTH, DFF, D])

    yout = dram_out("yout", [128, NCH, TOKC])
    mk_o = dram_out("mk", [DEPTH, NMEM, D])
    mv_o = dram_out("mv", [DEPTH, NMEM, D])
    ca_p = dram_out("ca_p", [DEPTH, 128, NCH, 30])
    cb_p = dram_out("cb_p", [DEPTH, 128, NCH, 3])
    lh_p = dram_out("lh_p", [DEPTH, 128, NCH])
    ca_s = dram_out("ca_s", [DEPTH, 128, NCH, NSEQ_S, 30])
    cb_s = dram_out("cb_s", [DEPTH, 128, NCH, NSEQ_S, 3])
    lh_s = dram_out("lh_s", [DEPTH, 128, NCH, NSEQ_S])

    X = Buf(nc, "X", NCH, TOK, F32)
    U = Buf(nc, "U", NCH, TOK, BF16)
    B = Buf(nc, "B", 24, BCOLS, BF16)
    S = Buf(nc, "S", 1, NCH * TOK, F32, page=64)
    RING = Buf(nc, "RING", NSLOT, SLOT_ELEMS, BF16, page=SLOT_ELEMS)
    DIAGA = Buf(nc, "DIAGA", 1, 31 * 128, BF16, page=128)
    DIAGB = Buf(nc, "DIAGB", 1, 4 * 128, BF16, page=4 * 128)
    AGS = Buf(nc, "AGS", NCH, SQG * 38, BF16, page=SQG * 38)
    BXS = Buf(nc, "BXS", NCH, SQG * 11, BF16, page=SQG * 11)
    LRW = Buf(nc, "LRW", 2, NCH * 128, BF16, page=128)
    SQ = Buf(nc, "SQ", 4, 512, BF16, page=512)
    RSTD = Buf(nc, "RSTD", 2, 512, F32, page=512)
    TMP = Buf(nc, "TMP", 4, 512, F32, page=512)
    CONST = Buf(nc, "CONST", NCH, NROW, F32, page=NCH * NROW)
    DER = Buf(nc, "DER", 1, 96, F32, page=96)
    IDB = Buf(nc, "IDB", 1, 128, BF16, page=128)
    IDF = Buf(nc, "IDF", 1, 128, F32, page=128)
    ONES = Buf(nc, "ONES", 1, 128, BF16, page=128)
    AGF = Buf(nc, "AGF", NCH, SG + 30, F32, page=SG + 30)
    BXF = Buf(nc, "BXF", NCH, SG + 3, F32, page=SG + 3)
    HLB = Buf(nc, "HLB", NCH, 16, F32, page=16)
    H0B = Buf(nc, "H0B", NCH, 8, F32, page=8)
    HLP = Buf(nc, "HLP", 1, NCH, F32, page=NCH)
    CARA = Buf(nc, "CARA", DEPTH * NCH, 30, BF16, page=30)
    CARB = Buf(nc, "CARB", DEPTH * NCH, 4, BF16, page=4)
    CARH = Buf(nc, "CARH", DEPTH, NCH, F32, page=NCH)
    SMALL = Buf(nc, "SMALL", 1, 64, F32, page=8)

    psum = [nc.alloc_psum_tensor(f"ps{i}", [128, 512], F32) for i in range(8)]
    pstate = {'i': 0, 'res': set()}

    def bank():
        while True:
            i = pstate['i']
            pstate['i'] = (i + 1) % 8
            if i not in pstate['res']:
                return (psum[i], [('ps', i)])

    def reserve(n):
        bs = [bank() for _ in range(n)]
        for b in bs:
            pstate['res'].add(b[1][0][1])
        return bs

    def release(bs):
        for b in bs:
            pstate['res'].discard(b[1][0][1])

    def act(out, in_, func, bias=None, scale=None, accum=None, extra_r=()):
        kw = {}
        if bias is not None:
            kw['bias'] = bias[0] if isinstance(bias, tuple) else bias
        if scale is not None:
            kw['scale'] = scale[0] if isinstance(scale, tuple) else scale
        if accum is not None:
            kw['accum_out'] = accum[0]
        rds = [in_] + [x for x in (bias, scale) if isinstance(x, tuple)] + list(extra_r)
        wrs = [out] + ([accum] if accum is not None else [])
        P.op('act', (lambda e, o=out[0], i=in_[0], f=func, k=kw: e.activation(out=o, in_=i, func=f, **k)),
             reads=rds, writes=wrs)

    def tt(eng, out, in0, in1, op):
        P.op(eng, (lambda e, o=out[0], a=in0[0], b=in1[0], p=op: e.tensor_tensor(out=o, in0=a, in1=b, op=p)),
             reads=[in0, in1], writes=[out])

    def stt(eng, out, in0, scalar, in1, op0, op1):
        sc = scalar[0] if isinstance(scalar, tuple) else scalar
        rds = [in0, in1] + ([scalar] if isinstance(scalar, tuple) else [])
        P.op(eng, (lambda e, o=out[0], a=in0[0], s=sc, b=in1[0], p0=op0, p1=op1:
                   e.scalar_tensor_tensor(out=o, in0=a, scalar=s, in1=b, op0=p0, op1=p1)),
             reads=rds, writes=[out])

    def ts(eng, out, in0, s1, op0, s2=None, op1=None):
        a1 = s1[0] if isinstance(s1, tuple) else s1
        a2 = s2[0] if isinstance(s2, tuple) else s2
        rds = [in0] + [x for x in (s1, s2) if isinstance(x, tuple)]
        if op1 is None:
            fn = (lambda e, o=out[0], a=in0[0]: e.tensor_scalar(out=o, in0=a, scalar1=a1, scalar2=None, op0=op0))
        else:
            fn = (lambda e, o=out[0], a=in0[0]: e.tensor_scalar(out=o, in0=a, scalar1=a1, scalar2=a2, op0=op0, op1=op1))
        P.op(eng, fn, reads=rds, writes=[out])

    def cp(eng, out, in_):
        if eng == 'act':
            act(out, in_, AF.Copy)
        else:
            P.op(eng, (lambda e, o=out[0], i=in_[0]: e.tensor_copy(out=o, in_=i)), reads=[in_], writes=[out])

    def recip(out, in_, fast=False):
        if fast:
            P.op('dve', (lambda e, o=out[0], i=in_[0]: e.reciprocal_approx_fast(out=o, in_=i)), reads=[in_], writes=[out])
        else:
            P.op('dve', (lambda e, o=out[0], i=in_[0]: e.reciprocal(out=o, in_=i)), reads=[in_], writes=[out])

    def memset(eng, out, val):
        P.op(eng, (lambda e, o=out[0], v=val: e.memset(o, v)), writes=[out])

    def mm(out, lhsT, rhs, start, stop, inc=None):
        if inc is None:
            inc = stop
        P.op('pe', (lambda e, o=out[0], l=lhsT[0], r=rhs[0], s0=start, s1=stop:
                    e.matmul(o, lhsT=l, rhs=r, start=s0, stop=s1)),
             reads=[lhsT, rhs], writes=[out], inc=inc)

    def transpose(out, in_, ident):
        P.op('pe', (lambda e, o=out[0], i=in_[0], d=ident[0]: e.transpose(out=o, in_=i, identity=d)),
             reads=[in_, ident], writes=[out], inc=True)

    def sl(reg, *idx):
        return (reg[0][idx], reg[1])

    def cst(c, row):
        ap, k = CONST.f(c * NROW + row, 1)
        return (ap, k)

    def der(i):
        return DER.f(i, 1)

    D_EPS, D_ONE = 0, 1
    def D_GH(l, i, c): return 2 + (l * 2 + i) * 8 + c
    def D_CL(l, c): return 40 + l * 8 + c
    def D_CL2(l, c): return 60 + l * 8 + c
    D_TMP = 80

    tmp_state = {'i': 0, 'sq': 0, 'rs': 0}

    def tmp(n):
        i = tmp_state['i']
        tmp_state['i'] = (i + 1) % 4
        return TMP.r(i, 0, n)

    def sqt(n):
        i = tmp_state['sq']
        tmp_state['sq'] = (i + 1) % 4
        return SQ.r(i, 0, n)

    def rst(n):
        i = tmp_state['rs']
        tmp_state['rs'] = (i + 1) % 2
        return RSTD.r(i, 0, n)

    class WS:
        def __init__(self):
            self.plan = []
            self.slots = [P.slot(f"w{i}") for i in range(NSLOT)]
            self.issued = 0
            self.consumed = 0

        def get(self, wap, l, c0, cc, kk):
            desc = (wap, l, c0, cc, kk)
            if P.planning:
                self.plan.append(desc)
                return None
            i = self.consumed
            self.consumed += 1
            pd = self.plan[i]
            assert pd[1:] == desc[1:] and pd[0] is wap, (i, pd[1:], desc[1:])
            lim = min(len(self.plan), i + NSLOT - 1)
            while self.issued < lim:
                self._issue(self.issued)
                self.issued += 1
            s = i % NSLOT
            reg = RING.g(s * SLOT_ELEMS, kk * cc, "p (k m) -> p k m", m=cc)
            return reg

        def _issue(self, j):
            wap, l, c0, cc, kk = self.plan[j]
            s = j % NSLOT
            reg = RING.g(s * SLOT_ELEMS, kk * cc, "p (k m) -> p k m", m=cc)
            src = wap[l][:, c0:c0 + cc].rearrange("(k p) m -> p k m", p=128)
            P.dma('pool', self.slots[s], reg[0], src, writes=[reg])

    ws = WS()

    def wv(W, k, m0, mc):
        return (W[0][:, k, m0:m0 + mc], W[1])

    def Yr(c, c0, n):
        return S.f(c * TOK + c0, n)

    def init_consts():
        s_c = P.slot("s_const")
        P.dma('sp', s_c, CONST.t[:, :].rearrange("p (c r) -> p c r", r=NROW), consts_d, writes=[CONST.f(0, NCH * NROW)])
        idf = IDF.f(0, 128)
        memset('pool', idf, 0.0)
        P.op('pool', (lambda e, o=idf[0]: e.affine_select(out=o, in_=o, pattern=[[-1, 128]], compare_op=ALU.not_equal,
                                                            fill=1.0, base=0, channel_multiplier=1)),
             reads=[idf], writes=[idf])
        cp('dve', IDB.f(0, 128), idf)
        memset('dve', ONES.f(0, 128), 1.0 / 1024.0)
        memset('dve', der(D_EPS), EPS)
        memset('dve', der(D_ONE), 1.0)
        for l in range(DEPTH):
            for i, gi in enumerate((1, 8)):
                for c in range(NCH):
                    ts('dve', der(D_GH(l, i, c)), cst(c, R_NG(l, gi)), 0.5, ALU.mult)
            for c in range(NCH):
                t1 = der(D_TMP)
                act(t1, cst(c, R_LAM(l)), AF.Exp, scale=-1.0)
                act(t1, t1, AF.Ln, bias=der(D_ONE))
                ts('dve', der(D_CL(l, c)), t1, -8.0, ALU.mult)
                ts('dve', der(D_CL2(l, c)), t1, -16.0, ALU.mult)
        memset('dve', CARA.f(0, DEPTH * NCH * 30), 0.0)
        memset('dve', CARB.f(0, DEPTH * NCH * 4), 0.0)
        memset('dve', CARH.f(0, DEPTH * NCH), 0.0)

    def rmsnorm_U(l, gi, tiles):
        for T in tiles:
            c0, n = T.c0, T.n
            ms = bank()
            for c in range(NCH):
                sq = sqt(n)
                act(sq, X.r(c, c0, n), AF.Square)
                mm(sl(ms, slice(None), slice(0, n)), ONES.f(0, 128), sq, c == 0, c == NCH - 1, inc=True)
            rs = rst(n)
            act(rs, sl(ms, slice(None), slice(0, n)), AF.Ln, bias=der(D_EPS))
            act(rs, rs, AF.Exp, scale=-0.5)
            for c in range(NCH):
                stt('dve', U.r(c, c0, n), X.r(c, c0, n), cst(c, R_NG(l, gi)), rs, ALU.mult, ALU.mult)

    pend_stats = []

    def flush_stats(keep=0):
        while len(pend_stats) > keep:
            st, n, sq, m = pend_stats.pop(0)
            mm(sl(st, slice(None), slice(0, n)), ONES.f(0, 128), sq, m == 0, m == NCH - 1, inc=True)

    def evac_y(m, T, pyn, stats):
        flush_stats(keep=2)
        act(Yr(m, T.c0, T.n), pyn, AF.Copy)
        sq = sqt(T.n)
        act(sq, pyn, AF.Square)
        pend_stats.append((stats[T.idx], T.n, sq, m))

    def post_residual(l, gscale_fn, tiles, stats):
        flush_stats()
        for T in tiles:
            c0, n = T.c0, T.n
            ms = stats[T.idx]
            rs = rst(n)
            act(rs, sl(ms, slice(None), slice(0, n)), AF.Ln, bias=der(D_EPS))
            act(rs, rs, AF.Exp, scale=-0.5)
            for c in range(NCH):
                t = tmp(n)
                tt('dve', t, Yr(c, c0, n), rs, ALU.mult)
                stt('dve', X.r(c, c0, n), t, gscale_fn(c), X.r(c, c0, n), ALU.mult, ALU.add)

    def proj_out(wap, l, in_row_fn, tiles, kk=NCH, cc=256):
        nsl = D // cc
        per = cc // 128
        ntail = 2

        def group(W, mi, m, T):
            py = bank()
            pyn = sl(py, slice(None), slice(0, T.n))
            for k in range(kk):
                mm(pyn, wv(W, k, mi * 128, 128), in_row_fn(k, T), k == 0, k == kk - 1)
            return pyn
        for s_ in range(nsl - ntail):
            W = ws.get(wap, l, s_ * cc, cc, kk)
            if P.planning:
                continue
            for mi in range(per):
                m = s_ * per + mi
                for T in tiles:
                    yield m, T, group(W, mi, m, T)
        Wt = [(s_, ws.get(wap, l, s_ * cc, cc, kk)) for s_ in range(nsl - ntail, nsl)]
        if P.planning:
            return
        for T in tiles:
            for s_, W in Wt:
                for mi in range(per):
                    m = s_ * per + mi
                    yield m, T, group(W, mi, m, T)

    def ffn(l, wi, wo, gpre, gpost_i, tiles):
        rmsnorm_U(l, gpre, tiles)
        for j2 in range(NFF // 2):
            Wg = ws.get(wi, l, 256 * j2, 256, NCH)
            Wu = ws.get(wi, l, DFF + 256 * j2, 256, NCH)
            if P.planning:
                continue
            for jj in range(2):
                j = 2 * j2 + jj
                for T in tiles:
                    c0, n = T.c0, T.n
                    pg = bank()
                    pu = bank()
                    pgn = sl(pg, slice(None), slice(0, n))
                    pun = sl(pu, slice(None), slice(0, n))
                    for k in range(NCH):
                        mm(pgn, wv(Wg, k, jj * 128, 128), U.r(k, c0, n), k == 0, k == NCH - 1)
                    for k in range(NCH):
                        mm(pun, wv(Wu, k, jj * 128, 128), U.r(k, c0, n), k == 0, k == NCH - 1)
                    t = tmp(n)
                    act(t, pgn, AF.Silu)
                    tt('dve', B.r(j, BPAD + c0, n), t, pun, ALU.mult)
        stats = reserve(3)
        for m, T, pyn in proj_out(wo, l, lambda k, T: B.r(k, BPAD + T.c0, T.n), tiles, kk=NFF, cc=128):
            if P.planning:
                continue
            evac_y(m, T, pyn, stats)
        if not P.planning:
            post_residual(l, lambda c: der(D_GH(l, gpost_i, c)), tiles, stats)
        release(stats)


    s_x = P.slot("s_x")
    s_y = P.slot("s_y")
    out_slots = [s_y]

    def scan(out, a, u, init):
        ini = init[0] if isinstance(init, tuple) else init
        rds = [a, u] + ([init] if isinstance(init, tuple) else [])
        P.op('dve', (lambda e, o=out[0], d0=a[0], d1=u[0], i=ini: e.tensor_tensor_scan(
            out=o, data0=d0, data1=d1, initial=i, op0=ALU.mult, op1=ALU.add)), reads=rds, writes=[out])

    def ags(c):
        return AGS.g(c * SQG * 38, SQG * 38, "p (s t) -> p s t", t=38)

    def bxs(c):
        return BXS.g(c * SQG * 11, SQG * 11, "p (s t) -> p s t", t=11)

    def v3(reg, t=DS):
        return (reg[0].rearrange("p (s t) -> p s t", t=t), reg[1])

    def proj_w_in(l, blk, tiles):
        for s_ in range(4):
            W = ws.get(w_in, l, blk * D + s_ * 256, 256, NCH)
            if P.planning:
                continue
            for mi in range(2):
                m = s_ * 2 + mi
                for T in tiles:
                    ps = bank()
                    psn = sl(ps, slice(None), slice(0, T.n))
                    for k in range(NCH):
                        mm(psn, wv(W, k, mi * 128, 128), U.r(k, T.c0, T.n), k == 0, k == NCH - 1)
                    yield m, T, psn

    s_sa = P.slot("s_sa"); s_sb = P.slot("s_sb"); s_sh = P.slot("s_sh"); s_lrw = P.slot("s_lrw")
    s_o1 = P.slot("s_o1"); s_o2 = P.slot("s_o2"); s_o3 = P.slot("s_o3"); s_cp = P.slot("s_cp")
    out_slots.extend([s_o1, s_o2, s_o3, s_cp])

    SL_XF, SL_A2M = 0, 1

    def Sf(slot, T):
        return S.f(slot * TOK + T.c0, T.n)

    def XFBr(T):
        a = 1024 + T.c0 // 2
        n2 = T.n // 2
        return (TMP.t[:, a:a + n2].bitcast(BF16), TMP.keys(a, n2))

    def mixer(l, g, tiles):
        last = (g == n_groups - 1)
        SIGA, AGL, SA, YAG, GEL, BXR, SIGB = 0, 8, 16, 8, 0, 16, 16
        PTs = [T for T in tiles if T.kind == 'P']
        ST = [T for T in tiles if T.kind == 'S'][0]
        if not P.planning:
            rmsnorm_U(l, 2, tiles)
            sq0 = g * SQG
            P.dma_group('pool', s_sa, [(ags(c)[0][:, :, 0:30], sca[l][:, c, sq0:sq0 + SQG, :]) for c in range(NCH)],
                        writes=[AGS.f(0, NCH * SQG * 38)])
            P.dma_group('pool', s_sb, [(bxs(c)[0][:, :, 0:3], scb[l][:, c, sq0:sq0 + SQG, :]) for c in range(NCH)],
                        writes=[BXS.f(0, NCH * SQG * 11)])
            P.dma('sp', s_sh, H0B.t[:, :].rearrange("p (c s) -> p c s", s=SQG), slh[l][:, :, sq0:sq0 + SQG],
                  writes=[H0B.f(0, NCH * 8)])
            P.dma_group('pool', s_lrw, [
                (LRW.t[:, 0:NCH * 128].rearrange("p (n j) -> p n j", j=128), lru_w_a[l].rearrange("n i j -> i n j")),
                (LRW.t[:, NCH * 128:2 * NCH * 128].rearrange("p (n j) -> p n j", j=128), lru_w_x[l].rearrange("n i j -> i n j"))],
                writes=[LRW.f(0, 2 * NCH * 128)])
            if g == 0:
                P.dma('sp', s_cp, ca_s[l][:, :, :, 0:22], sca[l][:, :, :, 8:30])
            for c in range(NCH):
                cp('dve', B.r(AGL + c, 0, 30), CARA.r(l * NCH + c, 0, 30))
        for s_ in range(4):
            W1 = ws.get(w_in, l, s_ * 256, 256, NCH)
            W2 = ws.get(w_in, l, D + s_ * 256, 256, NCH)
            if P.planning:
                continue
            for mi in range(2):
                m = s_ * 2 + mi
                for T in tiles:
                    c0, n = T.c0, T.n
                    p1 = bank(); p2 = bank()
                    p1n = sl(p1, slice(None), slice(0, n)); p2n = sl(p2, slice(None), slice(0, n))
                    for k in range(NCH):
                        mm(p1n, wv(W1, k, mi * 128, 128), U.r(k, c0, n), k == 0, k == NCH - 1)
                    for k in range(NCH):
                        mm(p2n, wv(W2, k, mi * 128, 128), U.r(k, c0, n), k == 0, k == NCH - 1)
                    t = tmp(n)
                    act(t, p2n, AF.Sigmoid, bias=cst(m, R_BIN(l, 1)))
                    b1 = cst(m, R_BIN(l, 0))
                    if T.kind == 'P':
                        stt('dve', B.r(AGL + m, BPAD + c0, n), p1n, b1, t, ALU.add, ALU.mult)
                        if last and T is PTs[-1]:
                            stt('dve', AGF.r(m, SG, 30), sl(p1n, slice(None), slice(n - 30, n)), b1,
                                sl(t, slice(None), slice(n - 30, n)), ALU.add, ALU.mult)
                    else:
                        a3 = ags(m)
                        stt('dve', (a3[0][:, :, 30:38], a3[1]), v3(p1n), b1, v3(t), ALU.add, ALU.mult)
                        stt('dve', AGF.r(m, 0, SG), p1n, b1, t, ALU.add, ALU.mult)
        ga_gen = proj_w_in(l, 4, tiles)
        if P.planning:
            for _ in ga_gen:
                pass

        def ga_slabs(k):
            for _ in range(k * 2 * len(tiles)):
                m_, T_, psn_ = next(ga_gen)
                act(B.r(SIGA + m_, BPAD + T_.c0, T_.n), psn_, AF.Sigmoid, bias=cst(m_, R_BIN(l, 4)))
        if not P.planning:
            for c in range(NCH):
                cp('act', CARA.r(l * NCH + c, 0, 30), B.r(AGL + c, PG, 30))
            agf3 = AGF.t[:, :].rearrange("p (c x) -> p c x", x=SG + 30)
            P.dma_group('sp', s_o1, [(ca_s[l][:, c, g * SQG:(g + 1) * SQG, 22:30],
                                      agf3[:, c, 0:SG].rearrange("p (s t) -> p s t", t=DS)) for c in range(NCH)],
                        reads=[AGF.f(0, NCH * (SG + 30))])
            if last:
                P.dma('sp', s_o1, ca_p[l], agf3[:, :, SG:SG + 30], reads=[AGF.f(0, NCH * (SG + 30))])
            for c in range(NCH):
                a0 = c * NROW + R_CAW(l, 0)
                halves = []
                for (w0, w1) in ((0, 16), (16, 31)):
                    nw = w1 - w0
                    dgh = DIAGA.g(w0 * 128, nw * 128, "p (w j) -> p w j", j=128)
                    P.op('dve', (lambda e, o=dgh[0], i0=IDB.t[:, 0:128].unsqueeze(1).to_broadcast([128, nw, 128]),
                                 i1=CONST.t[:, a0 + w0:a0 + w1].unsqueeze(2).to_broadcast([128, nw, 128]):
                                 e.tensor_tensor(out=o, in0=i0, in1=i1, op=ALU.mult)),
                         reads=[IDB.f(0, 128), CONST.f(0, 1)], writes=[dgh])
                    halves.append(dgh)
                pcs = [bank() for _ in tiles]
                for w in range(31):
                    dgw = DIAGA.f(w * 128, 128)
                    for ti, T in enumerate(tiles):
                        c0, n = T.c0, T.n
                        pcn = sl(pcs[ti], slice(None), slice(0, n))
                        if T.kind == 'P':
                            rhs = B.r(AGL + c, c0 + w, n)
                        else:
                            a3 = ags(c)
                            rhs = (a3[0][:, :, w:w + DS], a3[1])
                        mm(pcn, dgw, rhs, w == 0, w == 30, inc=(w == 30 or w == 15))
                for ti, T in enumerate(tiles):
                    act(Yr(c, T.c0, T.n), sl(pcs[ti], slice(None), slice(0, T.n)), AF.Identity, bias=cst(c, R_CAB(l)))
            for T in tiles:
                c0, n = T.c0, T.n
                pm = bank(); pq = bank()
                pmn = sl(pm, slice(None), slice(0, n)); pqn = sl(pq, slice(None), slice(0, n))
                for c in range(NCH):
                    sq = sqt(n)
                    act(sq, Yr(c, c0, n), AF.Square)
                    mm(pqn, ONES.f(0, 128), sq, c == 0, c == NCH - 1, inc=True)
                    cb = sqt(n)
                    cp('dve', cb, Yr(c, c0, n))
                    mm(pmn, ONES.f(0, 128), cb, c == 0, c == NCH - 1, inc=True)
                mean = tmp(n)
                cp('act', mean, pmn)
                m2 = tmp(n)
                tt('dve', m2, mean, mean, ALU.mult)
                tt('dve', m2, pqn, m2, ALU.subtract)
                rs = rst(n)
                act(rs, m2, AF.Ln, bias=der(D_EPS))
                act(rs, rs, AF.Exp, scale=-0.5)
                nmr = rst(n)
                stt('dve', nmr, mean, -1.0, rs, ALU.mult, ALU.mult)
                for c in range(NCH):
                    t = tmp(n)
                    tt('dve', t, Yr(c, c0, n), rs, ALU.mult)
                    tt('dve', t, t, nmr, ALU.add)
                    act(B.r(SA + c, BPAD + c0, n), t, AF.Silu, scale=cst(c, R_LNG(l)), bias=cst(c, R_LNB(l)))
                ga_slabs(2 if T.idx == 1 else 1)
        for m, T, pyn in proj_out(w_a_out, l, lambda k, T: B.r(SA + k, BPAD + T.c0, T.n), tiles):
            tt('dve', B.r(YAG + m, BPAD + T.c0, T.n), pyn, B.r(SIGA + m, BPAD + T.c0, T.n), ALU.mult)
        for m, T, psn in proj_w_in(l, 3, tiles):
            act(B.r(GEL + m, BPAD + T.c0, T.n), psn, AF.Gelu_apprx_tanh, bias=cst(m, R_BIN(l, 3)))
        if not P.planning:
            for c in range(NCH):
                cp('dve', B.r(BXR + c, 27, 3), CARB.r(l * NCH + c, 0, 3))
        for m, T, psn in proj_w_in(l, 2, tiles):
            bb = cst(m, R_BIN(l, 2))
            n = T.n
            if T.kind == 'P':
                act(B.r(BXR + m, BPAD + T.c0, n), psn, AF.Identity, bias=bb)
                if last and T is PTs[-1]:
                    act(BXF.r(m, SG, 3), sl(psn, slice(None), slice(n - 3, n)), AF.Identity, bias=bb)
            else:
                b3 = bxs(m)
                act((b3[0][:, :, 3:11], b3[1]), v3(psn), AF.Identity, bias=bb)
                act(BXF.r(m, 0, SG), psn, AF.Identity, bias=bb)
        if not P.planning:
            for c in range(NCH):
                cp('act', CARB.r(l * NCH + c, 0, 3), B.r(BXR + c, 27 + PG, 3))
            bxf3 = BXF.t[:, :].rearrange("p (c x) -> p c x", x=SG + 3)
            P.dma_group('sp', s_o2, [(cb_s[l][:, c, g * SQG:(g + 1) * SQG, :],
                                      bxf3[:, c, 0:SG].rearrange("p (s t) -> p s t", t=DS)[:, :, 5:8]) for c in range(NCH)],
                        reads=[BXF.f(0, NCH * (SG + 3))])
            if last:
                P.dma('sp', s_o2, cb_p[l], bxf3[:, :, SG:SG + 3], reads=[BXF.f(0, NCH * (SG + 3))])
            SL_A2M_ = 0

            def slots(c):
                p = c % 2
                return 1 + p, 3 + p, 5 + p

            conv_banks = {}

            def lru_P(c):
                dgb = DIAGB.g(0, 4 * 128, "p (w j) -> p w j", j=128)
                a0 = c * NROW + R_CBW(l, 0)
                P.op('dve', (lambda e, o=dgb[0], i0=IDB.t[:, 0:128].unsqueeze(1).to_broadcast([128, 4, 128]),
                             i1=CONST.t[:, a0:a0 + 4].unsqueeze(2).to_broadcast([128, 4, 128]):
                             e.tensor_tensor(out=o, in0=i0, in1=i1, op=ALU.mult)),
                     reads=[IDB.f(0, 128), CONST.f(0, 1)], writes=[dgb])
                bs = []
                for T in tiles:
                    c0, n = T.c0, T.n
                    pc = bank()
                    pcn = sl(pc, slice(None), slice(0, n))
                    for w in range(4):
                        if T.kind == 'P':
                            rhs = B.r(BXR + c, 27 + c0 + w, n)
                        else:
                            b3 = bxs(c)
                            rhs = (b3[0][:, :, w:w + DS], b3[1])
                        mm(pcn, (dgb[0][:, w, :], dgb[1]), rhs, w == 0, w == 3)
                    bs.append(pcn)
                conv_banks[c] = bs

            def lru_A(c):
                SL_XH, SL_RA, SL_IU = slots(c)
                for ti, T in enumerate(tiles):
                    ts('dve', Sf(SL_XH, T), conv_banks[c][ti], cst(c, R_CBB(l)), ALU.add)
                cp('dve', (TMP.t[:, 1024:1024 + TOK // 2].bitcast(BF16), TMP.keys(1024, TOK // 2)), S.f(SL_XH * TOK, TOK))

            lru_banks = {}

            def lru_C(c):
                bs = []
                for T in tiles:
                    n = T.n
                    pr = bank(); pi = bank()
                    prn = sl(pr, slice(None), slice(0, n)); pin = sl(pi, slice(None), slice(0, n))
                    mm(pin, LRW.f(NCH * 128 + c * 128, 128), XFBr(T), True, True)
                    mm(prn, LRW.f(c * 128, 128), XFBr(T), True, True)
                    bs.append((prn, pin))
                lru_banks[c] = bs

            def lru_D(c):
                SL_XH, SL_RA, SL_IU = slots(c)
                for ti, T in enumerate(tiles):
                    prn, pin = lru_banks[c][ti]
                    act(Sf(SL_IU, T), pin, AF.Sigmoid, bias=cst(c, R_LBX(l)))
                for ti, T in enumerate(tiles):
                    prn, pin = lru_banks[c][ti]
                    act(Sf(SL_RA, T), prn, AF.Sigmoid, bias=cst(c, R_LBA(l)))

            TALL = Tile(9, 0, TOK, 'A')
            TPR = Tile(8, 0, PG, 'P')

            def lru_E(c):
                SL_XH, SL_RA, SL_IU = slots(c)
                tt('dve', Sf(SL_IU, TALL), Sf(SL_IU, TALL), Sf(SL_XH, TALL), ALU.mult)

            def lru_Fa(c):
                SL_XH, SL_RA, SL_IU = slots(c)
                act(Sf(SL_RA, TALL), Sf(SL_RA, TALL), AF.Exp, scale=der(D_CL(l, c)))

            def lru_Fb(c):
                SL_XH, SL_RA, SL_IU = slots(c)
                tt('dve', Sf(SL_A2M_, TALL), Sf(SL_RA, TALL), Sf(SL_RA, TALL), ALU.mult)

            def lru_Fc(c):
                act(Sf(SL_A2M_, TALL), Sf(SL_A2M_, TALL), AF.Sqrt, scale=-1.0, bias=der(D_ONE))

            def lru_G(c):
                SL_XH, SL_RA, SL_IU = slots(c)
                tt('dve', Sf(SL_IU, TALL), Sf(SL_IU, TALL), Sf(SL_A2M_, TALL), ALU.mult)
                hc = CARH.f(l * NCH + c, 1)
                scan(Sf(SL_XH, TPR), Sf(SL_RA, TPR), Sf(SL_IU, TPR), hc)
                cp('dve', hc, sl(Sf(SL_XH, TPR), slice(None), slice(PG - 1, PG)))
                if last:
                    cp('act', HLP.f(c, 1), sl(Sf(SL_XH, TPR), slice(None), slice(PG - 1, PG)))
                T = ST
                A3 = v3(Sf(SL_RA, T)); U3 = v3(Sf(SL_IU, T)); H3 = v3(Sf(SL_XH, T))
                t8 = SMALL.f(0, 8)
                a30 = (A3[0][:, :, 0], A3[1]); u30 = (U3[0][:, :, 0], U3[1])
                tt('dve', t8, a30, H0B.r(c, 0, SQG), ALU.mult)
                tt('dve', u30, u30, t8, ALU.add)
                memset('dve', a30, 0.0)
                scan(Sf(SL_XH, T), Sf(SL_RA, T), Sf(SL_IU, T), 0.0)
                cp('act', HLB.r(c, 0, SQG), (H3[0][:, :, DS - 1], H3[1]))
                hg = B.r(GEL + c, BPAD, TOK)
                tt('dve', hg, Sf(SL_XH, TALL), hg, ALU.mult)

            lru_P(0); lru_A(0); lru_C(0)
            for c in range(NCH):
                lru_D(c)
                if c + 1 < NCH:
                    lru_P(c + 1)
                lru_E(c)
                lru_Fa(c)
                lru_Fb(c)
                if c + 1 < NCH:
                    lru_A(c + 1)
                    lru_C(c + 1)
                lru_Fc(c)
                lru_G(c)
            hl3 = HLB.t[:, :].rearrange("p (c x) -> p c x", x=16)
            P.dma('sp', s_o3, lh_s[l][:, :, g * SQG:(g + 1) * SQG], hl3[:, :, 0:SQG], reads=[HLB.f(0, NCH * 16)])
            if last:
                P.dma('sp', s_o3, lh_p[l], HLP.t[:, 0:NCH], reads=[HLP.f(0, NCH)])
        for m, T, psn in proj_w_in(l, 5, tiles):
            act(B.r(SIGB + m, BPAD + T.c0, T.n), psn, AF.Sigmoid, bias=cst(m, R_BIN(l, 5)))
        for m, T, pyn in proj_out(w_b_out, l, lambda k, T: B.r(GEL + k, BPAD + T.c0, T.n), tiles):
            t = tmp(T.n)
            tt('dve', t, pyn, B.r(SIGB + m, BPAD + T.c0, T.n), ALU.mult)
            yg = B.r(YAG + m, BPAD + T.c0, T.n)
            tt('dve', yg, t, yg, ALU.add)
        stats = reserve(3)
        for m, T, pyn in proj_out(w_out, l, lambda k, T: B.r(YAG + k, BPAD + T.c0, T.n), tiles):
            evac_y(m, T, pyn, stats)
        if not P.planning:
            post_residual(l, lambda c: cst(c, R_NG(l, 3)), tiles, stats)
        release(stats)


    MASK = Buf(nc, "MASK", 1, SQG * SG, BF16, page=SQG * SG)
    s_mem = P.slot("s_mem"); s_kt = [P.slot("s_kt0"), P.slot("s_kt1")]; s_vv = [P.slot("s_v0"), P.slot("s_v1")]
    s_ok = P.slot("s_ok")
    out_slots.append(s_ok)
    XS0 = 16 * BCOLS

    def Sb(a, n, pattern=None, parts=128, **kw):
        ap = S.t[0:parts, a:a + n].bitcast(BF16)
        if pattern is not None:
            ap = ap.rearrange(pattern, **kw)
        return (ap, S.keys(a, n))

    def bankbf(parts=128):
        b = bank()
        return (b[0][0:parts, :].bitcast(BF16), b[1])

    def init_mask():
        memset('dve', MASK.f(0, SQG * SG), 0.0)
        for b in range(SQG):
            memset('dve', MASK.f(b * SG + b * DS, DS), 1.0)

    SM2 = Buf(nc, "SM2", 1, 64, F32, page=1)
    SM3 = Buf(nc, "SM3", 1, 32, F32, page=1)

    def softmax_rows(ps, pm_out, ssum, parts, h=0):
        mx = SM3.f(h, 1, parts)
        nmx = SM3.f(16 + h, 1, parts)
        P.op('dve', (lambda e, o=mx[0], i=ps[0]: e.reduce_max(out=o, in_=i, axis=AX.X)), reads=[ps], writes=[mx])
        ts('dve', nmx, mx, -1.0 / 16.0, ALU.mult)
        act(pm_out, ps, AF.Exp, bias=nmx, scale=1.0 / 16.0, accum=ssum)

    def xattn(l, g, tiles):
        Q, OT = 0, 8
        PTs = [T for T in tiles if T.kind == 'P']
        ST = [T for T in tiles if T.kind == 'S'][0]
        KTP = B.g(XS0, 2048, "p (j m) -> p j m", m=NMEM)
        VP = B.g(XS0 + 2048, 2048, "p (c f) -> p c f", f=D)
        MT = B.g(XS0 + 4096, 2048, "p (c m) -> p c m", m=NMEM)
        PTb = B.f(XS0 + 6144, 256)
        PM = lambda parts: B.f(XS0 + 6400, 1024, parts)
        OTM = lambda parts: B.f(XS0 + 7424, 1024, parts)
        MEMF = S.g(0, NCH * NMEM, "p (c m) -> p c m", m=NMEM)
        KST = S.g(2048, 2048, "p (c f) -> p c f", f=D)
        if not P.planning:
            rmsnorm_U(l, 4, tiles)
            if 'mem' not in XA_SKIP:
                P.dma('sp', s_mem, MEMF[0], memT, writes=[MEMF])
            ms = bank()
            msn = sl(ms, slice(None), slice(0, NMEM))
            for c in range(NCH):
                sq = sqt(NMEM)
                act(sq, (MEMF[0][:, c, :], MEMF[1]), AF.Square)
                mm(msn, ONES.f(0, 128), sq, c == 0, c == NCH - 1, inc=True)
            rs = rst(NMEM)
            act(rs, msn, AF.Ln, bias=der(D_EPS))
            act(rs, rs, AF.Exp, scale=-0.5)
            for c in range(NCH):
                stt('dve', (MT[0][:, c, :], MT[1]), (MEMF[0][:, c, :], MEMF[1]), cst(c, R_NG(l, 6)), rs, ALU.mult, ALU.mult)
        for which, wap in ((0, xa_w_k), (1, xa_w_v)):
            for s_ in range(4):
                W = ws.get(wap, l, s_ * 256, 256, NCH)
                if P.planning or 'kv' in XA_SKIP:
                    continue
                if which == 0:
                    for mi in range(2):
                        j = s_ * 2 + mi
                        pk = bank()
                        pkn = sl(pk, slice(None), slice(0, NMEM))
                        for k in range(NCH):
                            mm(pkn, wv(W, k, mi * 128, 128), (MT[0][:, k, :], MT[1]), k == 0, k == NCH - 1)
                        cp('act', (KTP[0][:, j, :], KTP[1]), pkn)
                for mc in range(2):
                    if which == 0 and g != 0:
                        continue
                    pk = bank()
                    pkn = sl(pk, slice(None), slice(0, 256))
                    for k in range(NCH):
                        mm(pkn, (MT[0][:, k, mc * 128:(mc + 1) * 128], MT[1]), (W[0][:, k, :], W[1]), k == 0, k == NCH - 1)
                    if which == 1:
                        cp('act', (VP[0][:, mc, s_ * 256:(s_ + 1) * 256], VP[1]), pkn)
                    if g == 0:
                        cp('dve', (KST[0][:, mc, s_ * 256:(s_ + 1) * 256], KST[1]), pkn)
            if not P.planning and g == 0 and 'kvout' not in XA_SKIP and 'kv' not in XA_SKIP:
                dst = (mk_o if which == 0 else mv_o)[l].rearrange("(c p) f -> p c f", p=128)
                P.dma('sp', s_ok, dst, KST[0], reads=[KST])
        for m, T, pyn in proj_out(xa_w_q, l, lambda k, T: U.r(k, T.c0, T.n), tiles):
            cp('act', B.r(Q + m, BPAD + T.c0, T.n), pyn)
        if not P.planning:
            PTb8 = B.f(XS0 + 4096, 2048)
            PMb = [B.f(XS0 + 6144, 2048), Sb(0, 1024)]
            hb = [(bi, h) for bi in range(2) for h in range(4)]
            NPAIR = 0 if 'pattn' in XA_SKIP else PG // 256
            att = {}

            def att_qk(p):
                cols = [BPAD + (p * 2 + bi) * 128 for bi in range(2)]
                pss = [bank() for _ in range(4)]
                for i, (bi, h) in enumerate(hb):
                    psn = sl(pss[i // 2], slice(None), slice((i % 2) * 256, (i % 2) * 256 + 256))
                    for q2 in range(2):
                        mm(psn, B.r(Q + h * 2 + q2, cols[bi], 128), (KTP[0][:, h * 2 + q2, :], KTP[1]), q2 == 0, q2 == 1)
                att[p] = pss

            def att_softmax(p):
                pss = att[p]
                PM2 = PMb[p % 2]
                o3 = (p % 2) * 8
                mx8 = SM3.f(o3, 8)
                nmx8 = SM3.f(16 + o3, 8)
                for b_ in range(4):
                    src = (pss[b_][0][:, :].rearrange("p (a m) -> p a m", m=NMEM), pss[b_][1])
                    dst = SM3.f(o3 + 2 * b_, 2)
                    P.op('dve', (lambda e, o=dst[0], i=src[0]: e.reduce_max(out=o, in_=i, axis=AX.X)), reads=[src], writes=[dst])
                ts('dve', nmx8, mx8, -1.0 / 16.0, ALU.mult)
                for i, (bi, h) in enumerate(hb):
                    psn = sl(pss[i // 2], slice(None), slice((i % 2) * 256, (i % 2) * 256 + 256))
                    act(sl(PM2, slice(None), slice(i * 256, (i + 1) * 256)), psn, AF.Exp, bias=SM3.f(16 + o3 + i, 1),
                        scale=1.0 / 16.0, accum=SM2.f((p % 2) * 16 + i, 1))

            def att_transP(p):
                PM2 = PMb[p % 2]
                ptps = [bankbf(), bankbf()]
                for i in range(8):
                    for mc in range(2):
                        a = (i % 4) * 256 + mc * 128
                        transpose(sl(ptps[i // 4], slice(None), slice(a, a + 128)),
                                  sl(PM2, slice(None), slice(i * 256 + mc * 128, i * 256 + mc * 128 + 128)), IDB.f(0, 128))
                cp('dve', sl(PTb8, slice(None), slice(0, 1024)), ptps[0])
                cp('act', sl(PTb8, slice(None), slice(1024, 2048)), ptps[1])

            def att_pv(p):
                pos = [bank() for _ in range(4)]
                for i, (bi, h) in enumerate(hb):
                    pon = sl(pos[i // 2], slice(None), slice((i % 2) * 256, (i % 2) * 256 + 256))
                    for mc in range(2):
                        a = i * 256 + mc * 128
                        mm(pon, sl(PTb8, slice(None), slice(a, a + 128)),
                           (VP[0][:, mc, h * 256:(h + 1) * 256], VP[1]), mc == 0, mc == 1)
                att[('o', p)] = pos

            def att_out(p):
                PM2 = PMb[p % 2]
                pos = att[('o', p)]
                cols = [BPAD + (p * 2 + bi) * 128 for bi in range(2)]
                o0 = (p % 2) * 16
                rs8 = SM2.f(o0 + 8, 8)
                recip(rs8, SM2.f(o0, 8), fast=False)
                for i, (bi, h) in enumerate(hb):
                    pon = sl(pos[i // 2], slice(None), slice((i % 2) * 256, (i % 2) * 256 + 256))
                    dst = sl(PM2, slice(None), slice(i * 256, (i + 1) * 256))
                    if i % 2 == 0:
                        act(dst, pon, AF.Copy, scale=SM2.f(o0 + 8 + i, 1))
                    else:
                        ts('dve', dst, pon, SM2.f(o0 + 8 + i, 1), ALU.mult)
                otps = [bankbf(), bankbf()]
                for bi in range(2):
                    for j in range(NCH):
                        transpose(sl(otps[bi], slice(None), slice(j * 128, (j + 1) * 128)),
                                  sl(PM2, slice(None), slice(bi * 1024 + j * 128, bi * 1024 + (j + 1) * 128)), IDB.f(0, 128))
                for bi in range(2):
                    col = cols[bi]
                    for half in range(2):
                        j0 = half * 4
                        dst_ap = B.t[:, (OT + j0) * BCOLS:(OT + j0 + 4) * BCOLS].rearrange("p (r c) -> p r c", c=BCOLS)[:, :, col:col + 128]
                        dst_k = []
                        for j in range(j0, j0 + 4):
                            dst_k += B.keys((OT + j) * BCOLS + col, 128)
                        src = (otps[bi][0][:, j0 * 128:(j0 + 4) * 128].rearrange("p (r c) -> p r c", c=128), otps[bi][1])
                        cp('act' if half == 0 else 'dve', (dst_ap, dst_k), src)

            if NPAIR:
                att_qk(0)
                att_softmax(0)
            for p in range(NPAIR):
                if p + 1 < NPAIR:
                    att_qk(p + 1)
                att_transP(p)
                if p + 1 < NPAIR:
                    att_softmax(p + 1)
                att_pv(p)
                att_out(p)
            if 'sattn' not in XA_SKIP:
                QPAD = Sb(0, 2048, "p (j b t) -> p j b t", b=SQG, t=SG)
                PTPAD = Sb(2048, 2048, "p (h c b t) -> p h c b t", c=2, b=SQG, t=SG)
                KTs = [Sb(4096 + i * 1024, 1024, "p (j m) -> p j m", m=NMEM) for i in range(2)]
                Vs = [Sb(6144 + i * 1024, 1024, "p (c f) -> p c f", f=D) for i in range(2)]
                mask3 = MASK.g(0, SQG * SG, "p (b t) -> p b t", t=SG)
                scol = BPAD + ST.c0
                for j in range(NCH):
                    qs = B.r(Q + j, scol, SG)
                    tt('dve', (QPAD[0][:, j, :, :], QPAD[1]), (qs[0].unsqueeze(1).to_broadcast([128, SQG, SG]), qs[1]), mask3, ALU.mult)
                psh = [bank() for _ in range(4)]
                sq0 = g * SQG
                for b in range(SQG):
                    kt = KTs[b % 2]
                    P.dma('pool', s_kt[b % 2], kt[0], kcT[l, sq0 + b], writes=[kt])
                    for h in range(4):
                        for q2 in range(2):
                            mm((psh[h][0][0:SG, 0:NMEM], psh[h][1]), (QPAD[0][:, h * 2 + q2, b, :], QPAD[1]),
                               (kt[0][:, h * 2 + q2, :], kt[1]), b == 0 and q2 == 0, b == SQG - 1 and q2 == 1,
                               inc=(q2 == 1))
                for h in range(4):
                    ssum = SM2.f(h, 1, SG)
                    pmh = sl(PM(SG), slice(None), slice(h * 256, (h + 1) * 256))
                    softmax_rows((psh[h][0][0:SG, 0:NMEM], psh[h][1]), pmh, ssum, SG, h)
                    ptp = bankbf()
                    for mc in range(2):
                        transpose(sl(ptp, slice(None), slice(mc * SG, (mc + 1) * SG)),
                                  sl(pmh, slice(None), slice(mc * 128, (mc + 1) * 128)), (IDB.t[0:SG, 0:SG], IDB.keys(0, 128)))
                    for mc in range(2):
                        src = sl(ptp, slice(None), slice(mc * SG, (mc + 1) * SG))
                        tt('dve', (PTPAD[0][:, h, mc, :, :], PTPAD[1]),
                           (src[0].unsqueeze(1).to_broadcast([128, SQG, SG]), src[1]), mask3, ALU.mult)
                poh = [bank() for _ in range(4)]
                for b in range(SQG):
                    vb = Vs[b % 2]
                    P.dma('pool', s_vv[b % 2], vb[0], vc[l, sq0 + b], writes=[vb])
                    for h in range(4):
                        for mc in range(2):
                            mm((poh[h][0][0:SG, 0:256], poh[h][1]), (PTPAD[0][:, h, mc, b, :], PTPAD[1]),
                               (vb[0][:, mc, h * 256:(h + 1) * 256], vb[1]), b == 0 and mc == 0, b == SQG - 1 and mc == 1,
                               inc=(mc == 1))
                for h in range(4):
                    ssum = SM2.f(h, 1, SG)
                    rsm = SM2.f(8 + h, 1, SG)
                    recip(rsm, ssum, fast=False)
                    act(sl(OTM(SG), slice(None), slice(h * 256, (h + 1) * 256)), (poh[h][0][0:SG, 0:256], poh[h][1]),
                        AF.Copy, scale=rsm)
                otp = bankbf()
                for j in range(NCH):
                    transpose(sl(otp, slice(None), slice(j * SG, (j + 1) * SG)),
                              sl(OTM(SG), slice(None), slice(j * 128, (j + 1) * 128)), (IDB.t[0:SG, 0:SG], IDB.keys(0, 128)))
                for j in range(NCH):
                    cp('dve' if j % 2 else 'act', B.r(OT + j, scol, SG), sl(otp, slice(None), slice(j * SG, (j + 1) * SG)))
        stats = reserve(3)
        for m, T, pyn in proj_out(xa_w_o, l, lambda k, T: B.r(OT + k, BPAD + T.c0, T.n), tiles):
            evac_y(m, T, pyn, stats)
        if not P.planning:
            post_residual(l, lambda c: cst(c, R_NG(l, 5)), tiles, stats)
        release(stats)

    def emit_all():
        pstate['i'] = 0
        if not P.planning:
            init_consts()
            init_mask()
        for g in range(n_groups):
            tiles = TILES
            if not P.planning:
                for c in range(NCH):
                    pass
                P.dma('sp', s_x, X.t[:, :].rearrange("p (c t) -> p c t", t=TOK), xin[:, :, g * TOK:(g + 1) * TOK],
                      writes=[X.f(0, NCH * TOK)])
            for l in range(n_layers):
                if 'ffn1' in stages:
                    ffn(l, ffn1_w_in, ffn1_w_out, 0, 0, tiles)
                if 'mix' in stages:
                    mixer(l, g, tiles)
                if 'xa' in stages:
                    xattn(l, g, tiles)
                if 'ffn2' in stages:
                    ffn(l, ffn2_w_in, ffn2_w_out, 7, 1, tiles)
            if not P.planning:
                P.dma('sp', s_y, yout[:, :, g * TOK:(g + 1) * TOK], X.t[:, :].rearrange("p (c t) -> p c t", t=TOK),
                      reads=[X.f(0, NCH * TOK)])

    P.planning = True
    emit_all()
    P.planning = False
    emit_all()
    P.final_wait('sp', out_slots)
    P.emit()
    return nc


_WNAMES = ["w_in", "w_a_out", "w_b_out", "w_out", "lru_w_a", "lru_w_x", "xa_w_q", "xa_w_k", "xa_w_v", "xa_w_o",
           "ffn1_w_in", "ffn1_w_out", "ffn2_w_in", "ffn2_w_out"]


def _fm(a):
    sh = a.shape
    a = a.reshape(sh[:-2] + (sh[-2], NCH, 128))
    nd = a.ndim
    perm = tuple(range(nd - 3)) + (nd - 1, nd - 2, nd - 3)
    return np.ascontiguousarray(a.transpose(perm))


def _unfm(a):
    nd = a.ndim
    perm = tuple(range(nd - 3)) + (nd - 1, nd - 2, nd - 3)
    a = a.transpose(perm)
    return np.ascontiguousarray(a.reshape(a.shape[:-2] + (D,)))


def prepare_inputs(inp):
    f = lambda k: np.asarray(inp[k], dtype=np.float32)
    xp = f('x_prompt')
    xs = f('x_sample')
    mem = f('mem_prompt')
    ck = f('cache_mem_k')
    cv = f('cache_mem_v')
    sa = f('state_conv_a')
    sb = f('state_conv_b')
    sh = f('state_lru_h')
    rows = np.zeros((NROW, D), np.float32)
    ng = f('norm_g')
    for l in range(DEPTH):
        for i in range(9):
            rows[R_NG(l, i)] = ng[l, i]
        for j in range(6):
            rows[R_BIN(l, j)] = f('b_in')[l, j * D:(j + 1) * D]
        for w in range(31):
            rows[R_CAW(l, w)] = f('conv_a_w')[l, w]
        rows[R_CAB(l)] = f('conv_a_b')[l]
        rows[R_LNG(l)] = f('conv_ln_g')[l]
        rows[R_LNB(l)] = f('conv_ln_b')[l]
        for w in range(4):
            rows[R_CBW(l, w)] = f('conv_b_w')[l, w]
        rows[R_CBB(l)] = f('conv_b_b')[l]
        rows[R_LBA(l)] = f('lru_b_a')[l]
        rows[R_LBX(l)] = f('lru_b_x')[l]
        rows[R_LAM(l)] = f('lru_lambda')[l]
    consts = np.ascontiguousarray(rows.reshape(NROW, NCH, 128).transpose(2, 1, 0))
    wts = {}
    for k in _WNAMES:
        a = f(k)
        if k in ("xa_w_q", "xa_w_k", "xa_w_v", "xa_w_o"):
            a = a.reshape(DEPTH, D, D)
        wts[k] = np.ascontiguousarray(a)
    in_maps = []
    for c in range(NCORES):
        toks = []
        for g in range(NGRP):
            toks.append(xp[c, g * PG:(g + 1) * PG])
            s0 = c * NSEQ_S + g * SQG
            toks.append(xs[s0:s0 + SQG].reshape(SG, D))
        xt = np.concatenate(toks, axis=0)
        m = dict(wts)
        m["xin"] = _fm(xt)
        m["memT"] = _fm(mem[c])
        sl_ = slice(c * NSEQ_S, (c + 1) * NSEQ_S)
        k5 = ck[:, sl_]
        k5 = k5.reshape(DEPTH, NSEQ_S, NMEM, 4, 2, 128)
        m["kcT"] = np.ascontiguousarray(k5.transpose(0, 1, 5, 3, 4, 2).reshape(DEPTH, NSEQ_S, 128, 8, NMEM))
        v5 = cv[:, sl_].reshape(DEPTH, NSEQ_S, 2, 128, D)
        m["vc"] = np.ascontiguousarray(v5.transpose(0, 1, 3, 2, 4))
        a5 = sa[:, sl_].reshape(DEPTH, NSEQ_S, 30, NCH, 128)
        m["sca"] = np.ascontiguousarray(a5.transpose(0, 4, 3, 1, 2))
        b5 = sb[:, sl_].reshape(DEPTH, NSEQ_S, 3, NCH, 128)
        m["scb"] = np.ascontiguousarray(b5.transpose(0, 4, 3, 1, 2))
        h4 = sh[:, sl_].reshape(DEPTH, NSEQ_S, NCH, 128)
        m["slh"] = np.ascontiguousarray(h4.transpose(0, 3, 2, 1))
        m["consts"] = consts
        in_maps.append(m)
    return in_maps


def assemble_outputs(results):
    y_prompt = np.zeros((NCORES, SEQ, D), np.float32)
    y_sample = np.zeros((NCORES * NSEQ_S, DS, D), np.float32)
    mem_k = np.zeros((DEPTH, NCORES, NMEM, 4, 256), np.float32)
    mem_v = np.zeros((DEPTH, NCORES, NMEM, 4, 256), np.float32)
    ca_p = np.zeros((DEPTH, NCORES, 30, D), np.float32)
    cb_p = np.zeros((DEPTH, NCORES, 3, D), np.float32)
    lh_p = np.zeros((DEPTH, NCORES, D), np.float32)
    ca_s = np.zeros((DEPTH, NCORES * NSEQ_S, 30, D), np.float32)
    cb_s = np.zeros((DEPTH, NCORES * NSEQ_S, 3, D), np.float32)
    lh_s = np.zeros((DEPTH, NCORES * NSEQ_S, D), np.float32)
    for c in range(NCORES):
        r = results[c]
        yt = _unfm(np.asarray(r["yout"]))
        for g in range(NGRP):
            base = g * TOK
            y_prompt[c, g * PG:(g + 1) * PG] = yt[base:base + PG]
            s0 = c * NSEQ_S + g * SQG
            y_sample[s0:s0 + SQG] = yt[base + PG:base + TOK].reshape(SQG, DS, D)
        mem_k[:, c] = np.asarray(r["mk"]).reshape(DEPTH, NMEM, 4, 256)
        mem_v[:, c] = np.asarray(r["mv"]).reshape(DEPTH, NMEM, 4, 256)
        ca_p[:, c] = _unfm(np.asarray(r["ca_p"]))
        cb_p[:, c] = _unfm(np.asarray(r["cb_p"]))
        lh_p[:, c] = np.asarray(r["lh_p"]).transpose(0, 2, 1).reshape(DEPTH, D)
        sl_ = slice(c * NSEQ_S, (c + 1) * NSEQ_S)
        a = np.asarray(r["ca_s"])
        ca_s[:, sl_] = a.transpose(0, 3, 4, 2, 1).reshape(DEPTH, NSEQ_S, 30, D)
        b = np.asarray(r["cb_s"])
        cb_s[:, sl_] = b.transpose(0, 3, 4, 2, 1).reshape(DEPTH, NSEQ_S, 3, D)
        h = np.asarray(r["lh_s"])
        lh_s[:, sl_] = h.transpose(0, 3, 2, 1).reshape(DEPTH, NSEQ_S, D)
    return (y_prompt, y_sample, mem_k, mem_v, ca_p, cb_p, lh_p, ca_s, cb_s, lh_s)


def kernel(**inputs):
    in_maps = prepare_inputs(inputs)
    nc = build_program()
    res = run_bass_kernel_spmd(nc, in_maps, core_ids=list(range(NCORES)))
    return assemble_outputs(res.results)
```
